# Optimizing a Trainium2 kernel written in Bass

```python
import math
import jax, jax.numpy as jnp
from jax import lax
import numpy as np

D_MODEL = 2048
BATCH = 4
SEQ = 2048
DEPTH = 2

HEAD_DIM = 128
N_MIXERS = 2
DIL_GROUPS = ((128, 1), (512, 4), (2048, 16))
N_DIL_GROUPS = len(DIL_GROUPS)
DIL_HEADS = D_MODEL // HEAD_DIM
DIL_IN = (2 * N_DIL_GROUPS + 1) * D_MODEL
DIFF_HEADS = D_MODEL // (2 * HEAD_DIM)
D_FF = ((8 * D_MODEL // 3 + 255) // 256) * 256
ROPE_THETA = 10000.0
NORM_EPS = 1e-6
Q_BLOCK = 128
N_DIL_LAYERS = (DEPTH + 1) // 2
N_DIFF_LAYERS = DEPTH // 2

kernel_name = "hybrid_dilated_diff_macaron"


def rms_norm(x, g, eps=NORM_EPS):
    xf = x.astype(jnp.float32)
    y = xf * lax.rsqrt(jnp.mean(xf * xf, axis=-1, keepdims=True) + eps)
    return (y * g.astype(jnp.float32)).astype(x.dtype)


def rope_tables(seq):
    inv_freq = ROPE_THETA ** (-jnp.arange(0, HEAD_DIM, 2, dtype=jnp.float32) / HEAD_DIM)
    ang = jnp.arange(seq, dtype=jnp.float32)[:, None] * inv_freq[None, :]
    return jnp.cos(ang), jnp.sin(ang)


def apply_rope(x, cos, sin):
    shape = (x.shape[1],) + (1,) * (x.ndim - 3) + (cos.shape[-1],)
    c = cos.reshape(shape)
    s = sin.reshape(shape)
    xf = x.astype(jnp.float32)
    x1, x2 = jnp.split(xf, 2, axis=-1)
    return jnp.concatenate([x1 * c - x2 * s, x2 * c + x1 * s], axis=-1).astype(x.dtype)


def swiglu(h, w_gate_up, w_down):
    g, u = jnp.split(h @ w_gate_up, 2, axis=-1)
    return (jax.nn.silu(g) * u) @ w_down


def dilated_window_attention(q, k, v, window, dilation):
    B, S, H, Dh = q.shape
    d = dilation
    C = window // dilation
    span = C * d
    S_pad = -(-S // span) * span
    NB = S_pad // span
    pad = S_pad - S

    def to_blocks(t):
        t = jnp.pad(t, ((0, 0), (0, pad), (0, 0), (0, 0)))
        return t.reshape(B, NB, C, d, H, t.shape[-1])

    def with_prev(t):
        prev = jnp.pad(t, ((0, 0), (1, 0), (0, 0), (0, 0), (0, 0), (0, 0)))[:, :-1]
        return jnp.concatenate([prev, t], axis=2)

    qb = to_blocks(q) * (Dh ** -0.5)
    kk = with_prev(to_blocks(k))
    vv = with_prev(to_blocks(v))
    s = jnp.einsum('bnqrhd,bnkrhd->bnrhqk', qb, kk).astype(jnp.float32)
    qi = jnp.arange(C)[:, None]
    ki = jnp.arange(2 * C)[None, :]
    band = (ki >= qi) & (ki <= qi + C)
    blk = jnp.arange(NB)[:, None, None]
    valid = band[None] & ((blk > 0) | (ki[None] >= C))
    s = jnp.where(valid[None, :, None, None], s, -jnp.inf)
    lse = jax.nn.logsumexp(s, axis=-1)
    p = jnp.exp(s - lse[..., None])
    o = jnp.einsum('bnrhqk,bnkrhe->bnqrhe', p.astype(v.dtype), vv)
    o = o.reshape(B, S_pad, H, Dh)[:, :S]
    lse = jnp.transpose(lse, (0, 1, 4, 2, 3)).reshape(B, S_pad, H)[:, :S]
    return o, lse


def dilated_mixer(h, w_in, w_out, cos, sin):
    B, S, _ = h.shape
    proj = h @ w_in
    gd = N_DIL_GROUPS * D_MODEL
    q = apply_rope(proj[..., :gd].reshape(B, S, N_DIL_GROUPS, DIL_HEADS, HEAD_DIM), cos, sin)
    k = apply_rope(proj[..., gd:2 * gd].reshape(B, S, N_DIL_GROUPS, DIL_HEADS, HEAD_DIM), cos, sin)
    v = proj[..., 2 * gd:].reshape(B, S, DIL_HEADS, HEAD_DIM)
    outs, lses = [], []
    for g, (window, dilation) in enumerate(DIL_GROUPS):
        o, l = dilated_window_attention(q[:, :, g], k[:, :, g], v, window, dilation)
        outs.append(o)
        lses.append(l)
    wts = jax.nn.softmax(jnp.stack(lses, axis=0), axis=0)
    o = jnp.einsum('gbsh,gbshd->bshd', wts, jnp.stack(outs, axis=0).astype(jnp.float32))
    return o.astype(h.dtype).reshape(B, S, D_MODEL) @ w_out


def diff_mixer(h, w_in, w_out, lam_params, subln_gain, lam_init, cos, sin):
    B, S, _ = h.shape
    q, k, v = jnp.split(h @ w_in, 3, axis=-1)
    q = apply_rope(q.reshape(B, S, DIFF_HEADS, 2, HEAD_DIM), cos, sin) * (HEAD_DIM ** -0.5)
    k = apply_rope(k.reshape(B, S, DIFF_HEADS, 2, HEAD_DIM), cos, sin)
    v = v.reshape(B, S, DIFF_HEADS, 2 * HEAD_DIM)
    lp = lam_params.astype(jnp.float32)
    lam = jnp.exp(jnp.sum(lp[0] * lp[1])) - jnp.exp(jnp.sum(lp[2] * lp[3])) + lam_init
    nb = S // Q_BLOCK
    qb = jnp.moveaxis(q, 3, 0).reshape(2, B, nb, Q_BLOCK, DIFF_HEADS, HEAD_DIM)
    qb = jnp.moveaxis(qb, 2, 0)
    kk = jnp.moveaxis(k, 3, 0)
    kpos = jnp.arange(S)

    def block(args):
        qblk, start = args
        s = jnp.einsum('cbqhd,cbkhd->cbhqk', qblk, kk).astype(jnp.float32)
        qpos = start + jnp.arange(Q_BLOCK)
        s = jnp.where(qpos[:, None] >= kpos[None, :], s, -jnp.inf)
        p = jax.nn.softmax(s, axis=-1)
        a = p[0] - lam * p[1]
        return jnp.einsum('bhqk,bkhe->bqhe', a.astype(v.dtype), v)

    o = lax.map(block, (qb, jnp.arange(nb) * Q_BLOCK))
    o = jnp.moveaxis(o, 0, 1).reshape(B, S, DIFF_HEADS, 2 * HEAD_DIM)
    o = rms_norm(o, subln_gain, eps=1e-5) * (1.0 - lam_init)
    return o.reshape(B, S, D_MODEL) @ w_out


def setup_inputs(seed: int = 0) -> dict:
    key = jax.random.key(seed)
    ks = jax.random.split(key, 12)
    f32 = jnp.float32
    D = D_MODEL
    x = jax.random.normal(ks[0], (BATCH, SEQ, D), f32)
    norms = 1.0 + 0.05 * jax.random.normal(ks[1], (DEPTH, 6, D), f32)
    ffn_w_gate_up = jax.random.normal(ks[2], (DEPTH, 2, D, 2 * D_FF), f32) * D ** -0.5
    ffn_w_down = jax.random.normal(ks[3], (DEPTH, 2, D_FF, D), f32) * D_FF ** -0.5
    dil_w_in = jax.random.normal(ks[4], (N_DIL_LAYERS, D, DIL_IN), f32) * D ** -0.5
    dil_w_out = jax.random.normal(ks[5], (N_DIL_LAYERS, D, D), f32) * D ** -0.5
    diff_w_in = jax.random.normal(ks[6], (N_DIFF_LAYERS, D, 3 * D), f32) * D ** -0.5
    diff_w_out = jax.random.normal(ks[7], (N_DIFF_LAYERS, D, D), f32) * D ** -0.5
    diff_lambda = 0.1 * jax.random.normal(ks[8], (N_DIFF_LAYERS, 4, HEAD_DIM), f32)
    diff_subln = 1.0 + 0.05 * jax.random.normal(ks[9], (N_DIFF_LAYERS, 2 * HEAD_DIM), f32)
    return {"x": x, "norms": norms, "ffn_w_gate_up": ffn_w_gate_up, "ffn_w_down": ffn_w_down,
            "dil_w_in": dil_w_in, "dil_w_out": dil_w_out, "diff_w_in": diff_w_in,
            "diff_w_out": diff_w_out, "diff_lambda": diff_lambda, "diff_subln": diff_subln}


def reference(x, norms, ffn_w_gate_up, ffn_w_down, dil_w_in, dil_w_out, diff_w_in,
              diff_w_out, diff_lambda, diff_subln):
    cos, sin = rope_tables(x.shape[1])
    for i in range(DEPTH):
        nm = norms[i]
        h = rms_norm(x, nm[0])
        x = x + 0.5 * rms_norm(swiglu(h, ffn_w_gate_up[i, 0], ffn_w_down[i, 0]), nm[1])
        h = rms_norm(x, nm[2])
        j = i // N_MIXERS
        if i % N_MIXERS == 0:
            m = dilated_mixer(h, dil_w_in[j], dil_w_out[j], cos, sin)
        else:
            lam_init = 0.8 - 0.6 * math.exp(-0.3 * i)
            m = diff_mixer(h, diff_w_in[j], diff_w_out[j], diff_lambda[j], diff_subln[j],
                           lam_init, cos, sin)
        x = x + rms_norm(m, nm[3])
        h = rms_norm(x, nm[4])
        x = x + 0.5 * rms_norm(swiglu(h, ffn_w_gate_up[i, 1], ffn_w_down[i, 1]), nm[5])
    return x
```

```python
import math
from bisect import bisect_right
from collections import defaultdict

import numpy as np
import concourse.bass as bass
import concourse.mybir as mybir
from concourse.bass_utils import run_bass_kernel_spmd

F32 = mybir.dt.float32
BF16 = mybir.dt.bfloat16
AF = mybir.ActivationFunctionType
ALU = mybir.AluOpType

D = 2048
NC = 16
T = 1024
TB = 512
NTB = 2
DFF = 5632
NJ = 44
HD = 128
EPS = 1e-6
SCALE = HD ** -0.5
SAME_ENGINE_SYNC = True
NSLOT = 3
SEM_EPOCH = 1500
SLOT_BYTES = 8192


class View:
    __slots__ = ("ap", "space", "ivals")

    def __init__(self, ap, space, ivals):
        self.ap = ap
        self.space = space
        self.ivals = ivals


def _ivals(shape, idx, itemsize, base):
    n = len(shape)
    idx = tuple(idx) + (slice(None),) * (n - len(idx))
    rngs = []
    for d, s in zip(shape, idx):
        if isinstance(s, int):
            rngs.append((s, s + 1, 1))
        else:
            a, b, st = s.indices(d)
            rngs.append((a, b, st))
    strides = [1] * n
    for i in range(n - 2, -1, -1):
        strides[i] = strides[i + 1] * shape[i + 1]
    k = n - 1
    while k > 0 and rngs[k] == (0, shape[k], 1):
        k -= 1
    out = []

    def rec(dim, off):
        a, b, st = rngs[dim]
        if dim == k:
            last = a + ((b - a - 1) // st) * st
            lo = off + a * strides[dim]
            hi = off + last * strides[dim] + strides[dim]
            out.append((lo, hi))
            return
        for v in range(a, b, st):
            rec(dim + 1, off + v * strides[dim])

    rec(0, 0)
    out.sort()
    merged = []
    for lo, hi in out:
        if merged and lo <= merged[-1][1]:
            merged[-1][1] = max(merged[-1][1], hi)
        else:
            merged.append([lo, hi])
    return [(base + lo * itemsize, base + hi * itemsize) for lo, hi in merged]


class Buf:
    def __init__(self, space, base_ap, lo, shape, dtype):
        self.space = space
        self.lo = lo
        self.shape = tuple(shape)
        self.itemsize = 4 if dtype == F32 else 2
        n = 1
        for s in shape:
            n *= s
        self.nbytes = n * self.itemsize
        ap = base_ap
        if len(shape) == 2:
            ap = ap.rearrange("p (a b) -> p a b", a=shape[0], b=shape[1])
        elif len(shape) == 3:
            ap = ap.rearrange("p (a b c) -> p a b c", a=shape[0], b=shape[1], c=shape[2])
        self.ap = ap

    def __getitem__(self, idx):
        if not isinstance(idx, tuple):
            idx = (idx,)
        ap = self.ap[(slice(None),) + idx]
        return View(ap, self.space, _ivals(self.shape, idx, self.itemsize, self.lo))

    def all(self):
        return View(self.ap, self.space, [(self.lo, self.lo + self.nbytes)])


class Op:
    __slots__ = ("eng", "fn", "deps", "kind", "signal", "sem", "count", "idx")

    def __init__(self, eng, fn, deps, kind, idx):
        self.eng = eng
        self.fn = fn
        self.deps = deps
        self.kind = kind
        self.signal = False
        self.sem = None
        self.count = 0
        self.idx = idx


class Prog:
    def __init__(self):
        self.ops = []
        self.recs = defaultdict(list)
        self.starts = defaultdict(list)
        self.dry = False

    def _access(self, space, lo, hi, i, key, is_write, deps):
        recs = self.recs[space]
        starts = self.starts[space]
        k = bisect_right(starts, lo) - 1
        if k < 0 or recs[k][1] <= lo:
            k += 1
        j = k
        while j < len(recs) and recs[j][0] < hi:
            r = recs[j]
            if r[2] is not None:
                deps.add(r[2])
            if is_write:
                deps.update(r[3].values())
            j += 1
        over = recs[k:j]
        pieces = []
        if is_write:
            if over and over[0][0] < lo:
                pieces.append([over[0][0], lo, over[0][2], dict(over[0][3])])
            pieces.append([lo, hi, i, {}])
            if over and over[-1][1] > hi:
                pieces.append([hi, over[-1][1], over[-1][2], dict(over[-1][3])])
        else:
            cur = lo
            for r in over:
                if r[0] > cur:
                    pieces.append([cur, r[0], None, {key: i}])
                r[3][key] = i
                pieces.append(r)
                cur = max(cur, r[1])
            if cur < hi:
                pieces.append([cur, hi, None, {key: i}])
        recs[k:j] = pieces
        starts[k:j] = [p[0] for p in pieces]

    def _record(self, eng, fn, reads, writes, kind):
        if self.dry:
            return
        i = len(self.ops)
        key = eng if kind == "c" else ("d", i)
        deps = set()
        for v in reads:
            for lo, hi in v.ivals:
                self._access(v.space, lo, hi, i, key, False, deps)
        for v in writes:
            for lo, hi in v.ivals:
                self._access(v.space, lo, hi, i, key, True, deps)
        deps.discard(i)
        self.ops.append(Op(eng, fn, deps, kind, i))

    def op(self, eng, fn, reads=(), writes=()):
        self._record(eng, fn, reads, writes, "c")

    def dma(self, q, out, in_, **kw):
        self._record(q, lambda e, o=out.ap, i=in_.ap, kw=kw: e.dma_start(out=o, in_=i, **kw), [in_], [out], "d")

    def emit(self, nc, block_cm, sems, ring_sems):
        ops = self.ops
        engs = {"pe": "tensor", "act": "scalar", "dve": "vector", "pool": "gpsimd", "sp": "sync"}
        qn = defaultdict(int)
        for o in ops:
            if o.kind in ("d", "cc"):
                ring = ring_sems[o.eng if o.kind == "d" else "cc"]
                n = qn[(o.eng, o.kind)]
                qn[(o.eng, o.kind)] += 1
                o.sem = ring[n % len(ring)]
                inc = 16 if o.kind == "d" else 1
                o.count = inc * (n // len(ring) + 1)
                o.signal = True
        for o in ops:
            for j in o.deps:
                d = ops[j]
                if d.kind == "c":
                    if d.eng == o.eng and o.kind == "c" and not (SAME_ENGINE_SYNC and d.eng != "pe"):
                        continue
                    d.signal = True
        cnt = defaultdict(int)
        for o in ops:
            if o.kind == "c" and o.signal:
                ep = cnt[o.eng] // SEM_EPOCH
                o.sem = sems[o.eng][ep]
                o.count = cnt[o.eng] % SEM_EPOCH + 1
                cnt[o.eng] += 1
        per = defaultdict(list)
        for o in ops:
            per[o.eng].append(o)
        self.stats = {"nops": len(ops), "cnt": dict(cnt), "per": {k: len(v) for k, v in per.items()},
                      "dmacnt": {k: v for k, v in qn.items()}}

        def make(engname):
            lst = per.get(engname, [])

            def body(e):
                waited = {}

                def wait(sem, val):
                    k = id(sem)
                    if waited.get(k, 0) >= val:
                        return
                    waited[k] = val
                    e.wait_ge(sem, val)

                for o in lst:
                    if o.kind in ("d", "cc"):
                        inc = 16 if o.kind == "d" else 1
                        if o.count > inc:
                            wait(o.sem, o.count - inc)
                    for j in sorted(o.deps):
                        d = ops[j]
                        if d.kind == "c" and d.eng == o.eng and o.kind == "c" and not (
                                SAME_ENGINE_SYNC and d.eng != "pe"):
                            continue
                        wait(d.sem, d.count)
                    if o.fn is None:
                        continue
                    ins = o.fn(e)
                    if o.signal:
                        if o.kind == "d":
                            ins.then_inc(o.sem, 16)
                        else:
                            ins.then_inc(o.sem, 1)
            return body

        with block_cm as block:
            for name, attr in engs.items():
                if per.get(name):
                    getattr(block, attr)(make(name))


LAM_INIT = 0.8 - 0.6 * math.exp(-0.3 * 1)


def build(stop=None, dbg=None, snap=False):
    nc = bass.Bass("TRN2", target_bir_lowering=False)
    P = Prog()

    def dram_in(name, shape, dt=F32):
        return nc.dram_tensor(name, list(shape), dt, kind="ExternalInput").ap()

    xT_d = dram_in("xT", [NC, 128, T])
    gains_d = dram_in("gains", [128, 12 * NC])
    cos_d = dram_in("cosT", [128, T])
    sin_d = dram_in("sinT", [128, T])
    flag_d = dram_in("flag", [128, 1])
    masks_d = dram_in("masks", [128, 4 * 128])
    perm_d = dram_in("perm", [128, 128])
    lam_d = dram_in("lamb", [128, 4 * 128])
    subln_d = dram_in("subln", [128, 2])
    if dbg is None:
        gu_d = [[dram_in(f"gu{l}{f}", [NJ, 128, 4096]) for f in range(2)] for l in range(2)]
        dn_d = [[dram_in(f"dn{l}{f}", [32, 128, 2816]) for f in range(2)] for l in range(2)]
    if dbg is None or dbg.startswith("dil"):
        dil_in_d = dram_in("dil_in", [56, 128, 4096])
        dil_out_d = dram_in("dil_out", [8, 128, 4096])
    if dbg is None or dbg.startswith("dif"):
        dif_in_d = dram_in("dif_in", [24, 128, 4096])
        dif_out_d = dram_in("dif_out", [8, 128, 4096])
    out_d = nc.dram_tensor("outT", [NC, 128, T], F32, kind="ExternalOutput").ap()
    snap_d = {}
    if snap:
        for nm_ in ("ffn00", "mix0", "layer0", "ffn10", "mix1"):
            snap_d[nm_] = nc.dram_tensor("snap_" + nm_, [NC, 128, T], F32, kind="ExternalOutput").ap()

    q1_d = nc.dram_tensor("q1_d", [48 * 128, T], BF16)
    k1_loc = nc.dram_tensor("k1_loc", [48 * 128, T], BF16)
    k1_all = [nc.dram_tensor(f"k1_all{i}", [2 * 512, T], BF16) for i in range(12)]
    v1_loc = nc.dram_tensor("v1_loc", [32 * 128, T], BF16)
    v1_all = [nc.dram_tensor(f"v1_all{i}", [2 * 512, T], BF16) for i in range(8)]
    q2_d = nc.dram_tensor("q2_d", [16 * 128, T], BF16)
    k2_loc = nc.dram_tensor("k2_loc", [16 * 128, T], BF16)
    k2_all = [nc.dram_tensor(f"k2_all{i}", [2 * 512, T], BF16) for i in range(4)]
    v2_loc = nc.dram_tensor("v2_loc", [8 * 128, 2048], BF16)
    v2_all = [nc.dram_tensor(f"v2_all{i}", [2 * 256, 2048], BF16) for i in range(4)]

    def dview(t, ap, lo=0, hi=1 << 40):
        return View(ap, "dram:" + t.name, [(lo, hi)])

    ARENA_BYTES = 207 * 1024
    cm_arena = nc.sbuf_tensor("arena", [128, ARENA_BYTES // 2], BF16)
    arena = cm_arena.__enter__()
    cms = [cm_arena]
    psum = []
    for b in range(8):
        cm = nc.psum_tensor(f"ps{b}", [128, 512], F32)
        psum.append(cm.__enter__())
        cms.append(cm)
    sems = {}
    for e, n in (("pe", 6), ("act", 4), ("dve", 5), ("pool", 2)):
        sems[e] = []
        for i in range(n):
            cm = nc.semaphore(f"sem_{e}{i}")
            sems[e].append(cm.__enter__())
            cms.append(cm)
    ring_sems = {}
    for q, n in (("sp", 40), ("pool", 12), ("cc", 24)):
        ring = []
        for i in range(n):
            cm = nc.semaphore(f"dq_{q}{i}")
            ring.append(cm.__enter__())
            cms.append(cm)
        ring_sems[q] = ring

    off = [0]

    def alloc(shape, dt, at=None):
        n = 1
        for s in shape:
            n *= s
        nb = n * (4 if dt == F32 else 2)
        lo = off[0] if at is None else at
        if at is None:
            off[0] += (nb + 31) // 32 * 32
        base = arena[:, lo // 2:(lo + nb) // 2]
        if dt == F32:
            base = base.bitcast(F32)
        return Buf("sb", base, lo, shape, dt)

    X = alloc([NC, T], F32)
    A = alloc([NC, TB], BF16)
    RB = off[0]
    off[0] += 44 * 1024
    RC = off[0]
    off[0] += 32 * 1024
    slots = [alloc([4096], BF16) for _ in range(NSLOT)]
    gains = alloc([12 * NC], F32)
    cosT = alloc([T], F32)
    sinT = alloc([T], F32)
    flag = alloc([1], F32)
    masks = alloc([4, 128], BF16)
    perm = alloc([128], F32)
    ones_f = alloc([128], F32)
    ones_b = alloc([128], BF16)
    fones_b = alloc([128], BF16)
    rstd = [alloc([TB], F32) for _ in range(2)]
    sq = [alloc([TB], F32) for _ in range(2)]
    tmpf = [alloc([TB], F32) for _ in range(3)]
    masks_f = alloc([4 * 128], F32, at=tmpf[0].lo)
    lamb = alloc([4, 128], F32, at=tmpf[1].lo)
    lamw = alloc([8], F32)
    subln = alloc([2], F32)
    assert off[0] <= ARENA_BYTES, off[0]
    HID = alloc([NJ, TB], BF16, at=RB)
    Y = alloc([NC, TB], F32, at=RC)
    YB = alloc([NC, TB], F32, at=RB)
    OT = alloc([NC, TB], BF16, at=RB + 60 * 1024)
    STG = alloc([4, TB], BF16, at=RB)
    VSTG = alloc([2, 4, 256], BF16, at=RB + 4096)

    PS = [Buf("ps", psum[b][:, :], b * 2048, [512], F32) for b in range(8)]

    class WS:
        def __init__(self):
            self.order = []
            self.dry = True
            self.pos = 0
            self.issued = 0

        def get(self, dram_ap, tname, n):
            if self.dry:
                self.order.append((dram_ap, tname, n))
                return slots[0]
            i = self.pos
            self.pos += 1
            while self.issued < min(len(self.order), i + NSLOT):
                ap, tn, nn = self.order[self.issued]
                s = slots[self.issued % NSLOT]
                P.dma("pool", s[0:nn], View(ap, "dram:" + tn, [(0, 1)]))
                self.issued += 1
            return slots[i % NSLOT]

    ws = WS()
    rr = defaultdict(int)

    def rot(name, lst):
        v = lst[rr[name] % len(lst)]
        rr[name] += 1
        return v

    MM_BANKS = [0, 1, 2, 3]

    def mm(out, lhsT, rhs, start, stop):
        P.op("pe", lambda e, o=out.ap, l=lhsT.ap, r=rhs.ap, s=start, t=stop: e.matmul(o, lhsT=l, rhs=r, start=s, stop=t),
             reads=[lhsT, rhs], writes=[out])

    def act(out, in_, func, reads_extra=(), **kw):
        kws = {k: (v.ap if isinstance(v, View) else v) for k, v in kw.items()}
        rd = [in_] + [v for v in kw.values() if isinstance(v, View)] + list(reads_extra)
        P.op("act", lambda e, o=out.ap, i=in_.ap, f=func, kws=kws: e.activation(out=o, in_=i, func=f, **kws),
             reads=rd, writes=[out])

    def tt(out, in0, in1, op, eng="dve"):
        P.op(eng, lambda e, o=out.ap, a=in0.ap, b=in1.ap, op=op: e.tensor_tensor(out=o, in0=a, in1=b, op=op),
             reads=[in0, in1], writes=[out])

    def ts(out, in0, s1, s2, op0, op1=None, eng="dve"):
        s1a = s1.ap if isinstance(s1, View) else s1
        s2a = s2.ap if isinstance(s2, View) else s2
        rd = [in0] + [v for v in (s1, s2) if isinstance(v, View)]
        if op1 is None:
            P.op(eng, lambda e, o=out.ap, a=in0.ap: e.tensor_single_scalar(out=o, in_=a, scalar=s1a, op=op0),
                 reads=rd, writes=[out])
        else:
            P.op(eng, lambda e, o=out.ap, a=in0.ap: e.tensor_scalar(out=o, in0=a, scalar1=s1a, scalar2=s2a, op0=op0, op1=op1),
                 reads=rd, writes=[out])

    def stt(out, in0, scalar, in1, op0, op1, eng="dve"):
        sa = scalar.ap if isinstance(scalar, View) else scalar
        rd = [in0, in1] + ([scalar] if isinstance(scalar, View) else [])
        P.op(eng, lambda e, o=out.ap, a=in0.ap, b=in1.ap: e.scalar_tensor_tensor(out=o, in0=a, scalar=sa, in1=b, op0=op0, op1=op1),
             reads=rd, writes=[out])

    def tsl(t):
        return slice(t * TB, (t + 1) * TB)

    def gcol(l, i, c):
        n = (l * 6 + i) * NC + c
        return gains[n:n + 1]

    def prologue():
        for c in range(NC):
            P.dma("sp", X[c], dview_in(xT_d[c], "xT"))
        P.dma("sp", gains.all(), dview_in(gains_d, "gains"))
        P.dma("sp", cosT.all(), dview_in(cos_d, "cos"))
        P.dma("sp", sinT.all(), dview_in(sin_d, "sin"))
        P.dma("sp", flag.all(), dview_in(flag_d, "flag"))
        P.dma("sp", masks_f.all(), dview_in(masks_d, "masks"))
        P.dma("sp", perm.all(), dview_in(perm_d, "perm"))
        P.dma("sp", lamb.all(), dview_in(lam_d, "lamb"))
        P.dma("sp", subln.all(), dview_in(subln_d, "subln"))
        P.op("dve", lambda e: e.memset(ones_f.ap, 1.0), writes=[ones_f.all()])
        P.op("dve", lambda e: e.memset(ones_b.ap, 1.0), writes=[ones_b.all()])
        ts(fones_b.all(), ones_f.all(), flag[0:1], None, ALU.mult)
        P.op("dve", lambda e: e.tensor_copy(out=masks.ap, in_=masks_f.ap.rearrange("p (a b) -> p a b", a=4, b=128)),
             reads=[masks_f.all()], writes=[masks.all()])
        ts(gains.all(), gains.all(), math.sqrt(D), None, ALU.mult)
        for l in range(2):
            for i in (1, 5):
                n = (l * 6 + i) * NC
                ts(gains[n:n + NC], gains[n:n + NC], 0.5, None, ALU.mult)
        for a in range(2):
            tt(tmpf[a][0:128], lamb[2 * a], lamb[2 * a + 1], ALU.mult)
            P.op("dve", lambda e, o=lamw[a:a + 1].ap, i=tmpf[a][0:128].ap: e.reduce_sum(out=o, in_=i, axis=mybir.AxisListType.X),
                 reads=[tmpf[a][0:128]], writes=[lamw[a:a + 1]])
            act(lamw[2 + a:3 + a], lamw[a:a + 1], AF.Exp)
        tt(lamw[4:5], lamw[3:4], lamw[2:3], ALU.subtract)
        ts(lamw[5:6], lamw[4:5], -LAM_INIT, None, ALU.add)
        ts(subln.all(), subln.all(), 16.0 * (1.0 - LAM_INIT), None, ALU.mult)

    def dview_in(ap, name):
        return View(ap, "dram:in_" + name, [(0, 1)])

    def stats_finish(ps_stat, r, dim, eps):
        act(r.all(), ps_stat.all(), AF.Sqrt, bias=float(dim * eps), scale=1.0)
        P.op("dve", lambda e, o=r.ap: e.reciprocal(out=o, in_=o), reads=[r.all()], writes=[r.all()])

    def sq_accumulate(r, src, first):
        if first:
            act(r.all(), src, AF.Square)
        else:
            s = rot("sq", sq)
            act(s.all(), src, AF.Square)
            tt(r.all(), r.all(), s.all(), ALU.add)

    pre_state = {}

    def prenorm_a(t, l, i):
        r = rot("rstd", rstd)
        pre_state[(t, l, i)] = r
        for c in range(NC):
            sq_accumulate(r, X[c, tsl(t)], c == 0)

    def prenorm_b(t, l, i):
        ps_stat = PS[4]
        r = pre_state.pop((t, l, i))
        mm(ps_stat.all(), ones_f.all(), r.all(), True, True)
        stats_finish(ps_stat, r, D, EPS)
        for c in range(NC):
            stt(A[c], X[c, tsl(t)], gcol(l, i, c), r.all(), ALU.mult, ALU.mult)

    def prenorm(t, l, i):
        prenorm_a(t, l, i)
        prenorm_b(t, l, i)

    norm_state = {}

    def evac_y(ps, Yb, c, l, i, ps_stat):
        act(Yb[c], ps.all(), AF.Copy, scale=gcol(l, i, c))
        if c == 0:
            norm_state["r"] = rot("rstd", rstd)
        sq_accumulate(norm_state["r"], ps.all(), c == 0)

    def postnorm(t, Yb, ps_stat):
        r = norm_state["r"]
        mm(ps_stat.all(), ones_f.all(), r.all(), True, True)
        stats_finish(ps_stat, r, D, EPS)
        for c in range(NC):
            tmp = rot("tmpf", tmpf)
            tt(tmp.all(), Yb[c], r.all(), ALU.mult)
            tt(X[c, tsl(t)], X[c, tsl(t)], tmp.all(), ALU.add)

    def ffn_ph1(t, l, f):
        for j in range(NJ):
            slot = ws.get(gu_d[l][f][j], f"gu{l}{f}", 4096)
            W = Buf("sb", slot.ap, slot.lo, [NC, 256], BF16)
            pg = PS[rot("mm", MM_BANKS)]
            pu = PS[rot("mm", MM_BANKS)]
            for k in range(NC):
                mm(pg.all(), W[k, 0:128], A[k], k == 0, k == NC - 1)
            for k in range(NC):
                mm(pu.all(), W[k, 128:256], A[k], k == 0, k == NC - 1)
            sl = rot("tmpf", tmpf)
            act(sl.all(), pg.all(), AF.Silu)
            tt(HID[j], sl.all(), pu.all(), ALU.mult)

    def ffn_ph2(t, l, f, mid=None):
        ps_stat = PS[5]
        for c in range(NC):
            if c == 3 and mid is not None:
                mid()
            py = PS[rot("mm", MM_BANKS)]
            for half in range(2):
                slot = ws.get(dn_d[l][f][c * 2 + half], f"dn{l}{f}", 2816)
                W = Buf("sb", slot.ap[:, 0:2816], slot.lo, [22, 128], BF16)
                for jj in range(22):
                    mm(py.all(), W[jj], HID[half * 22 + jj], half == 0 and jj == 0, half == 1 and jj == 21)
            evac_y(py, Y, c, l, 1 if f == 0 else 5, ps_stat)
        postnorm(t, Y, ps_stat)

    def ffn_seq(l, f, next_pre):
        pi = 0 if f == 0 else 4
        ffn_ph1(0, l, f)
        prenorm_a(1, l, pi)
        ffn_ph2(0, l, f, mid=lambda: prenorm_b(1, l, pi))
        ffn_ph1(1, l, f)
        if next_pre is not None:
            prenorm_a(*next_pre)
            ffn_ph2(1, l, f, mid=lambda: prenorm_b(*next_pre))
        else:
            ffn_ph2(1, l, f)

    def perm_out(stg_view_ap, g):
        if g == 0:
            return stg_view_ap, None
        return stg_view_ap.rearrange("p (r m) -> p m r", r=4, m=128), ("p (m r) -> p m r", dict(m=128, r=4))

    def qk_tile_finish(ps, t, g, dst_dram, dst_t, row0):
        qf = rot("tmpf", tmpf)
        act(qf.all(), ps.all(), AF.Copy)
        ps2 = PS[rot("swap", [6, 7])]
        mm(ps2.all(), perm.all(), qf.all(), True, True)
        t1 = rot("tmpf", tmpf)
        tt(t1.all(), qf.all(), cosT[tsl(t)], ALU.mult, eng="pool")
        t2 = rot("sq", sq)
        tt(t2.all(), ps2.all(), sinT[tsl(t)], ALU.mult)
        si = rr["stg"] % 4
        rr["stg"] += 1
        sv = STG[si]
        oap, inre = perm_out(sv.ap, g)
        if inre is None:
            a_ap, b_ap = t1.ap, t2.ap
        else:
            a_ap = t1.ap.rearrange(inre[0], **inre[1])
            b_ap = t2.ap.rearrange(inre[0], **inre[1])
        P.op("dve", lambda e, o=oap, a=a_ap, b=b_ap: e.tensor_tensor(out=o, in0=a, in1=b, op=ALU.add),
             reads=[t1.all(), t2.all()], writes=[sv])
        P.dma("sp", dview(dst_t, dst_dram[row0:row0 + 128, tsl(t)], (row0 * 2 + t) * 1, (row0 * 2 + t) * 1 + 1),
              sv)

    def v_cols(k, g, b):
        base = A[k]
        if g == 0:
            return A[k, b * 128:(b + 1) * 128]
        ap = base.ap.rearrange("p (m r) -> p r m", m=128, r=4)[:, b, :]
        return View(ap, "sb", base.ivals)

    def v_tile(slot, t, vt, v_loc, dil):
        W = Buf("sb", slot.ap, slot.lo, [NC, 256], BF16)
        for g in range(2 if dil else 1):
            vi = rr["vstg"] % 2
            rr["vstg"] += 1
            for pair in range(2):
                ps = PS[rot("mm", MM_BANKS)]
                for tb2 in range(2):
                    tb = pair * 2 + tb2
                    for k in range(NC):
                        mm(ps[tb2 * 256:(tb2 + 1) * 256], v_cols(k, g, tb), W[k], k == 0, k == NC - 1)
                P.op("act", lambda e, o=VSTG[vi, pair * 2:pair * 2 + 2].ap, i=ps.ap.rearrange("p (a b) -> p a b", a=2, b=256):
                     e.activation(out=o, in_=i, func=AF.Copy),
                     reads=[ps.all()], writes=[VSTG[vi, pair * 2:pair * 2 + 2]])
            if dil:
                for hh in range(2):
                    h = vt * 2 + hh
                    r0 = (g * 16 + h) * 128
                    dst = v_loc.ap()[r0:r0 + 128, t * TB:(t + 1) * TB].rearrange("p (b f) -> p b f", f=128)
                    src = VSTG[vi]
                    src = View(src.ap[:, :, hh * 128:(hh + 1) * 128], src.space, src.ivals)
                    P.dma("sp", dview(v_loc, dst, 100000 + r0 * 2 + t, 100000 + r0 * 2 + t + 1), src)
            else:
                r0 = vt * 128
                dst = v_loc.ap()[r0:r0 + 128, t * 4 * 256:(t + 1) * 4 * 256].rearrange("p (b f) -> p b f", f=256)
                P.dma("sp", dview(v_loc, dst, 100000 + r0 * 2 + t, 100000 + r0 * 2 + t + 1), VSTG[vi])

    def mixer_pass1(t, l, w_in_d, wname, n_qk_tiles, groups_of_tile, q_d, k_loc, v_loc, part="all", pre=True):
        if pre:
            prenorm(t, l, 2)
        pending = []
        tis = {"all": range(2 * n_qk_tiles), "q": range(n_qk_tiles), "kv": range(n_qk_tiles, 2 * n_qk_tiles)}[part]
        for ti in tis:
            slot = ws.get(w_in_d[ti], wname, 4096)
            W = Buf("sb", slot.ap, slot.lo, [NC, 256], BF16)
            isq = ti < n_qk_tiles
            tl = ti if isq else ti - n_qk_tiles
            for m in range(2):
                ps = PS[rot("mm", MM_BANKS)]
                for k in range(NC):
                    mm(ps.all(), W[k, m * 128:(m + 1) * 128], A[k], k == 0, k == NC - 1)
                row0 = (tl * 2 + m) * 128
                g = groups_of_tile(tl)
                dst_t = q_d if isq else k_loc
                for fn in pending:
                    fn()
                pending = [lambda ps=ps, g=g, dst_t=dst_t, row0=row0: qk_tile_finish(ps, t, g, dst_t.ap(), dst_t, row0)]
        for fn in pending:
            fn()
        if part == "q":
            return
        for vt in range(8):
            slot = ws.get(w_in_d[2 * n_qk_tiles + vt], wname, 4096)
            v_tile(slot, t, vt, v_loc, n_qk_tiles == 24)

    def exchange(k_loc, k_all, v_loc, v_all):
        nk = len(k_all)
        order = []
        for fc in range(4):
            if nk == 12:
                order += [("k", fc), ("k", 4 + fc), ("k", 8 + fc), ("v", fc), ("v", 4 + fc)]
            else:
                order += [("k", fc), ("v", fc)]
        vrows = 512 if nk == 12 else 256
        for kind, i in order:
            loc, a, rows = (k_loc, k_all[i], 512) if kind == "k" else (v_loc, v_all[i], vrows)
            P._record("pool", lambda e, x=loc.ap()[i * rows:(i + 1) * rows, :].opt(), y=a.ap().opt(): e.collective_compute(
                "AllGather", ALU.bypass, replica_groups=[[0, 1], [2, 3], [4, 5], [6, 7]], ins=[x], outs=[y]),
                [dview(loc, None)], [dview(a, None)], "cc")

    def attn_out_proj(t, l, w_out_d, wname):
        ps_stat = PS[5]
        for ti in range(8):
            slot = ws.get(w_out_d[ti], wname, 4096)
            W = Buf("sb", slot.ap, slot.lo, [NC, 256], BF16)
            for m in range(2):
                c = ti * 2 + m
                py = PS[rot("mm", MM_BANKS)]
                for k in range(NC):
                    mm(py.all(), W[k, m * 128:(m + 1) * 128], OT[k], k == 0, k == NC - 1)
                evac_y(py, YB, c, l, 3, ps_stat)
        postnorm(t, YB, ps_stat)

    AB = RB

    def dil_attention(t):
        nbk = (t + 1) * 4
        bufs = []
        o = AB
        for i in range(2):
            b = {}
            b["q"] = alloc([3, TB], BF16, at=o); o += 3 * 1024
            b["ko"] = alloc([3, T], BF16, at=o); o += 6 * 1024
            b["kp"] = alloc([3, T], BF16, at=o); o += 6 * 1024
            b["vo"] = alloc([2, 8, 128], BF16, at=o); o += 4 * 1024
            b["vp"] = alloc([2, 8, 128], BF16, at=o); o += 4 * 1024
            bufs.append(b)
        assert o <= RB + 54 * 1024
        EB = [alloc([TB], BF16, at=RB + 54 * 1024 + i * 1024) for i in range(6)]
        RZ = alloc([TB], F32, at=RB + 50 * 1024)

        def load_head(h, b):
            def own(tn, ncols, ng):
                return tn.ap().rearrange("(g h p) c -> p g h c", g=ng, h=16)[:, :, h, 0:ncols]
            P.dma("sp", b["q"].all(), dview(q1_d, q1_d.ap().rearrange("(g h p) c -> p g h c", g=3, h=16)[:, :, h, tsl(t)]))
            nt = (t + 1) * TB
            P.dma("sp", b["ko"][:, 0:nt], dview(k1_loc, own(k1_loc, nt, 3)))
            kcols = {0: (896, 1024) if t == 0 else None, 1: (512, 1024) if t == 0 else None, 2: (0, 1024)}
            for g in range(3):
                if kcols[g] is None:
                    continue
                c0, c1 = kcols[g]
                sl_ = g * 16 + h
                ka = k1_all[sl_ // 4]
                rs = slice((sl_ % 4) * 128, (sl_ % 4 + 1) * 128)
                P.dma("sp", b["kp"][g, c0:c1], dview(ka, ka.ap()[rs, c0:c1]))
            P.dma("sp", b["vo"][:, 0:(t + 1) * 4], dview(v1_loc, own(v1_loc, nt, 2).rearrange("p g (b f) -> p g b f", f=128)))
            vblk = {0: (7, 8) if t == 0 else None, 1: (0, 8)}
            for g in range(2):
                if vblk[g] is None:
                    continue
                b0, b1 = vblk[g]
                sl_ = g * 16 + h
                va = v1_all[sl_ // 4]
                rs = slice((sl_ % 4) * 128, (sl_ % 4 + 1) * 128)
                P.dma("sp", b["vp"][g, b0:b1], dview(va, va.ap()[rs, b0 * 128:b1 * 128].rearrange("p (b f) -> p b f", f=128)))
                ts(b["vp"][g, b0:b1], b["vp"][g, b0:b1], flag[0:1], None, ALU.mult, eng="pool")

        def pairs_for():
            pl = []
            for i in range(4):
                pl.append((0, i, "o", t * 4 + i, 0))
                if i > 0:
                    pl.append((0, i, "o", t * 4 + i - 1, 1))
                elif t == 1:
                    pl.append((0, i, "o", 3, 1))
                else:
                    pl.append((0, i, "p", 7, 1))
            for r in range(4):
                pl.append((1, r, "o", t * 4 + r, 0))
                if t == 1:
                    pl.append((1, r, "o", r, 1))
                else:
                    pl.append((1, r, "p", 4 + r, 1))
            for r4 in range(4):
                pl.append((2, r4, "p", r4, 2))
                pl.append((2, r4, "p", 4 + r4, 2))
                if t == 1:
                    pl.append((2, r4, "o", r4, 2))
                pl.append((2, r4, "o", t * 4 + r4, 3))
            return pl

        def qcols(ps, g, qb):
            if g == 0:
                v = ps[qb * 128:(qb + 1) * 128]
                return v.ap
            return ps.ap.rearrange("p (m r) -> p r m", m=128, r=4)[:, qb, :]

        pl = pairs_for()
        groups = [pl[i:i + 4] for i in range(0, len(pl), 4)]
        load_head(0, bufs[0])
        pending_norm = [None]
        for h in range(16):
            b = bufs[h % 2]
            if h + 1 < 16:
                load_head(h + 1, bufs[(h + 1) % 2])
            pso = PS[rot("dil_o", [2, 3])]
            psz = PS[rot("dil_z", [6, 7])]
            started = set()

            def do_pv(grp, E):
                for i, (g, qb, src, kb, mk) in enumerate(grp):
                    Ev = E[i * 128:(i + 1) * 128]
                    if mk is not None:
                        tt(Ev, Ev, masks[mk], ALU.mult)
                for i, (g, qb, src, kb, mk) in enumerate(grp):
                    Ev = E[i * 128:(i + 1) * 128]
                    st = len(started) == 0
                    started.add((g, qb))
                    V = b["vo"] if src == "o" else b["vp"]
                    on = ones_b if src == "o" else fones_b
                    rhs_ap = Ev.ap
                    vg = min(g, 1)
                    P.op("pe", lambda e, o=qcols(pso, g, qb), l=V[vg, kb].ap, r=rhs_ap, s=st:
                         e.matmul(o, lhsT=l, rhs=r, start=s, stop=False, skip_group_check=True),
                         reads=[V[vg, kb], Ev], writes=[pso.all()])
                    P.op("pe", lambda e, o=qcols(psz, g, qb), l=on.ap, r=rhs_ap, s=st:
                         e.matmul(o, lhsT=l, rhs=r, start=s, stop=False, skip_group_check=True),
                         reads=[on.all(), Ev], writes=[psz.all()])

            pend = []
            for gi, grp in enumerate(groups):
                pss = PS[rot("dil_s", [0, 1, 4, 5])]
                for i, (g, qb, src, kb, mk) in enumerate(grp):
                    K = b["ko"] if src == "o" else b["kp"]
                    mm(pss[i * 128:(i + 1) * 128], K[g, kb * 128:(kb + 1) * 128], b["q"][g, qb * 128:(qb + 1) * 128], True, True)
                E = rot("EB", EB)
                n = len(grp) * 128
                act(E[0:n], pss[0:n], AF.Exp, scale=SCALE)
                pend.append((grp, E))
                if len(pend) > 3:
                    do_pv(*pend.pop(0))
                if gi == 4 and pending_norm[0] is not None:
                    pending_norm[0]()
                    pending_norm[0] = None
            while pend:
                do_pv(*pend.pop(0))

            def norm(h=h, pso=pso, psz=psz):
                P.op("dve", lambda e, o=RZ.ap, i=psz.ap: e.reciprocal(out=o, in_=i), reads=[psz.all()], writes=[RZ.all()])
                tt(OT[h], pso.all(), RZ.all(), ALU.mult)
            pending_norm[0] = norm
        pending_norm[0]()
        pending_norm[0] = None

    def diff_attention(t):
        nown = (t + 1) * 4
        bufs = []
        o = AB
        for i in range(2):
            b = {}
            b["q"] = alloc([2, TB], BF16, at=o); o += 2 * 1024
            b["ko"] = alloc([2, T], BF16, at=o); o += 4 * 1024
            b["kp"] = alloc([2, T], BF16, at=o); o += 4 * 1024
            b["vo"] = alloc([8, 256], BF16, at=o); o += 4 * 1024
            b["vp"] = alloc([8, 256], BF16, at=o); o += 4 * 1024
            bufs.append(b)
        O1 = alloc([2, TB], F32, at=o); o += 4096
        RS = alloc([TB], F32, at=o); o += 2048
        assert o <= RB + 54 * 1024
        EB = [alloc([TB], BF16, at=RB + 54 * 1024 + i * 1024) for i in range(6)]
        RZ = alloc([TB], F32, at=RB + 50 * 1024)

        def load_head(h, b):
            nt = (t + 1) * TB
            P.dma("sp", b["q"].all(), dview(q2_d, q2_d.ap().rearrange("(h c p) t -> p h c t", h=8, c=2)[:, h, :, tsl(t)]))
            P.dma("sp", b["ko"][:, 0:nt], dview(k2_loc, k2_loc.ap().rearrange("(h c p) t -> p h c t", h=8, c=2)[:, h, :, 0:nt]))
            ka = k2_all[h // 2]
            P.dma("sp", b["kp"].all(), dview(ka, ka.ap()[(h % 2) * 256:(h % 2 + 1) * 256, :].rearrange("(c p) t -> p c t", c=2)))
            P.dma("sp", b["vo"][0:nown], dview(v2_loc, v2_loc.ap()[h * 128:(h + 1) * 128, 0:nown * 256].rearrange("p (b f) -> p b f", f=256)))
            va = v2_all[h // 2]
            P.dma("sp", b["vp"].all(), dview(va, va.ap()[(h % 2) * 128:(h % 2 + 1) * 128, :].rearrange("p (b f) -> p b f", f=256)))
            ts(b["vp"].all(), b["vp"].all(), flag[0:1], None, ALU.mult)

        kbl = [("p", kb) for kb in range(8)] + [("o", kb) for kb in range(nown)]
        load_head(0, bufs[0])
        for h in range(8):
            b = bufs[h % 2]
            if h + 1 < 8:
                load_head(h + 1, bufs[(h + 1) % 2])
            for c in range(2):
                psu = [PS[2], PS[3]] if c == 0 else [PS[6], PS[7]]
                psz = PS[4]

                def do_pv(item, E):
                    src, kb, q0, first = item
                    n = TB - q0
                    if src == "o" and kb >= t * 4:
                        tt(E[0:128], E[0:128], masks[0], ALU.mult)
                    V = b["vo"] if src == "o" else b["vp"]
                    on = ones_b if src == "o" else fones_b
                    for vc in range(2):
                        P.op("pe", lambda e, o=psu[vc][q0:TB].ap, l=V[kb, vc * 128:(vc + 1) * 128].ap, r=E[0:n].ap, s=first:
                             e.matmul(o, lhsT=l, rhs=r, start=s, stop=False, skip_group_check=True),
                             reads=[V[kb, vc * 128:(vc + 1) * 128], E[0:n]], writes=[psu[vc].all()])
                    P.op("pe", lambda e, o=psz[q0:TB].ap, l=on.ap, r=E[0:n].ap, s=first:
                         e.matmul(o, lhsT=l, rhs=r, start=s, stop=False, skip_group_check=True),
                         reads=[on.all(), E[0:n]], writes=[psz.all()])

                pend = []
                for idx, (src, kb) in enumerate(kbl):
                    q0 = 0
                    if src == "o" and kb >= t * 4:
                        q0 = (kb - t * 4) * 128
                    n = TB - q0
                    pss = PS[rot("dif_s", [0, 1, 5])]
                    K = b["ko"] if src == "o" else b["kp"]
                    mm(pss[0:n], K[c, kb * 128:(kb + 1) * 128], b["q"][c, q0:TB], True, True)
                    E = rot("EB", EB)
                    act(E[0:n], pss[0:n], AF.Exp, scale=SCALE)
                    pend.append(((src, kb, q0, idx == 0), E))
                    if len(pend) > 2:
                        do_pv(*pend.pop(0))
                while pend:
                    do_pv(*pend.pop(0))
                P.op("dve", lambda e, o=RZ.ap, i=psz.ap: e.reciprocal(out=o, in_=i), reads=[psz.all()], writes=[RZ.all()])
                if c == 0:
                    for vc in range(2):
                        tt(O1[vc], psu[vc].all(), RZ.all(), ALU.mult)
                else:
                    ts(RZ.all(), RZ.all(), lamw[5:6], None, ALU.mult)
                    for vc in range(2):
                        tmp = rot("tmpf", tmpf)
                        tt(tmp.all(), psu[vc].all(), RZ.all(), ALU.mult)
                        tt(O1[vc], O1[vc], tmp.all(), ALU.add)
            ps_stat = PS[4]
            for vc in range(2):
                s = rot("sq", sq)
                act(s.all(), O1[vc], AF.Square)
                mm(ps_stat.all(), ones_f.all(), s.all(), vc == 0, vc == 1)
            stats_finish(ps_stat, RS, 256, 1e-5)
            for vc in range(2):
                stt(OT[h * 2 + vc], O1[vc], subln[vc:vc + 1], RS.all(), ALU.mult, ALU.mult)

    def program():
        prologue()
        stage = 0

        def done(name):
            if name in snap_d:
                for c in range(NC):
                    P.dma("sp", View(snap_d[name][c], "dram:snap" + name, [(c, c + 1)]), X[c])
            return stop is not None and stop == name

        if dbg is not None:
            sub = dbg.split(":")[1] if ":" in dbg else "all"
            if dbg.startswith("dil"):
                for t in range(NTB):
                    mixer_pass1(t, 0, dil_in_d, "dil_in", 24, lambda tl: (tl * 2) // 16, q1_d, k1_loc, v1_loc)
                if sub == "p1":
                    return
                exchange(k1_loc, k1_all, v1_loc, v1_all)
                if sub == "xc":
                    return
                for t in range(NTB):
                    dil_attention(t)
                    attn_out_proj(t, 0, dil_out_d, "dil_out")
            else:
                for t in range(NTB):
                    mixer_pass1(t, 1, dif_in_d, "dif_in", 8, lambda tl: 0, q2_d, k2_loc, v2_loc)
                exchange(k2_loc, k2_all, v2_loc, v2_all)
                for t in range(NTB):
                    diff_attention(t)
                    attn_out_proj(t, 1, dif_out_d, "dif_out")
            return
        prenorm(0, 0, 0)
        ffn_seq(0, 0, (0, 0, 2))
        if done("ffn00"):
            return
        a1 = (0, dil_in_d, "dil_in", 24, lambda tl: (tl * 2) // 16, q1_d, k1_loc, v1_loc)
        mixer_pass1(0, *a1, part="kv", pre=False)
        mixer_pass1(1, *a1, part="kv")
        exchange(k1_loc, k1_all, v1_loc, v1_all)
        mixer_pass1(1, *a1, part="q", pre=False)
        mixer_pass1(0, *a1, part="q")
        dil_attention(0)
        attn_out_proj(0, 0, dil_out_d, "dil_out")
        prenorm_a(0, 0, 4)
        dil_attention(1)
        prenorm_b(0, 0, 4)
        attn_out_proj(1, 0, dil_out_d, "dil_out")
        if done("mix0"):
            return
        ffn_seq(0, 1, (0, 1, 0))
        if done("layer0"):
            return
        ffn_seq(1, 0, (0, 1, 2))
        if done("ffn10"):
            return
        a2 = (1, dif_in_d, "dif_in", 8, lambda tl: 0, q2_d, k2_loc, v2_loc)
        mixer_pass1(0, *a2, part="kv", pre=False)
        mixer_pass1(1, *a2, part="kv")
        exchange(k2_loc, k2_all, v2_loc, v2_all)
        mixer_pass1(1, *a2, part="q", pre=False)
        mixer_pass1(0, *a2, part="q")
        diff_attention(0)
        attn_out_proj(0, 1, dif_out_d, "dif_out")
        prenorm_a(0, 1, 4)
        diff_attention(1)
        prenorm_b(0, 1, 4)
        attn_out_proj(1, 1, dif_out_d, "dif_out")
        if done("mix1"):
            return
        ffn_seq(1, 1, None)

    ws.dry = True
    P.dry = True
    program()
    ws.dry = False
    P.dry = False
    rr.clear()
    program()
    outs = []
    for c in range(NC):
        P.dma("sp", View(out_d[c], "dram:out", [(c, c + 1)]), X[c])
    P.op("sp", None, reads=[View(None, "dram:out", [(0, NC)])])

    P.emit(nc, nc.Block(), sems, ring_sems)
    build.stats = P.stats
    for cm in reversed(cms):
        cm.__exit__(None, None, None)
    return nc


def _tile_kn(W, ncols=256):
    K, N = W.shape
    nt = N // ncols
    return np.ascontiguousarray(W.reshape(K // 128, 128, nt, ncols).transpose(2, 1, 0, 3)).reshape(nt, 128, (K // 128) * ncols)


def _prep_shared(norms, ffn_w_gate_up, ffn_w_down, dil_w_in, dil_w_out, diff_w_in, diff_w_out, diff_lambda, diff_subln):
    sh = {}
    g = np.asarray(norms, np.float32).reshape(12, NC, 128)
    sh["gains"] = np.ascontiguousarray(g.transpose(2, 0, 1)).reshape(128, 12 * NC)
    for l in range(2):
        for f in range(2):
            W = np.asarray(ffn_w_gate_up[l, f], np.float32)
            Wg = W[:, :DFF].reshape(D, NJ, 128)
            Wu = W[:, DFF:].reshape(D, NJ, 128)
            Wi = np.concatenate([Wg, Wu], axis=2).reshape(D, NJ * 256)
            sh[f"gu{l}{f}"] = _tile_kn(Wi)
            Wd = np.asarray(ffn_w_down[l, f], np.float32)
            Wd4 = Wd.reshape(2, 22, 128, NC, 128)
            sh[f"dn{l}{f}"] = np.ascontiguousarray(Wd4.transpose(3, 0, 2, 1, 4)).reshape(32, 128, 22 * 128)
    sh["dil_in"] = _tile_kn(np.asarray(dil_w_in[0], np.float32))
    sh["dil_out"] = _tile_kn(np.asarray(dil_w_out[0], np.float32))
    sh["dif_in"] = _tile_kn(np.asarray(diff_w_in[0], np.float32))
    sh["dif_out"] = _tile_kn(np.asarray(diff_w_out[0], np.float32))
    lp = np.asarray(diff_lambda[0], np.float32).reshape(1, 4 * 128)
    sh["lamb"] = np.ascontiguousarray(np.broadcast_to(lp, (128, 4 * 128)))
    sh["subln"] = np.ascontiguousarray(np.asarray(diff_subln[0], np.float32).reshape(2, 128).T)
    k = np.arange(128)[:, None]
    q = np.arange(128)[None, :]
    m_causal = (k <= q)
    m_upper = (k >= q)
    same = (k % 4) == (q % 4)
    m_bd = same
    m_bdc = same & ((k // 4) <= (q // 4))
    sh["masks"] = np.concatenate([m_causal, m_upper, m_bd, m_bdc], axis=1).astype(np.float32)
    pm = np.zeros((128, 128), np.float32)
    for m in range(128):
        pm[(m + 64) % 128, m] = 1.0
    sh["perm"] = pm
    return sh


def _prep_core(x, b, half):
    d = {}
    xs = np.asarray(x[b, half * T:(half + 1) * T, :], np.float32)
    d["xT"] = np.ascontiguousarray(xs.T).reshape(NC, 128, T)
    inv_freq = (np.float32(10000.0) ** (-np.arange(0, HD, 2, dtype=np.float32) / np.float32(HD))).astype(np.float32)
    pos = np.arange(half * T, (half + 1) * T, dtype=np.float32)
    ang = (pos[:, None] * inv_freq[None, :]).astype(np.float32)
    c = np.cos(ang).astype(np.float32).T
    s = np.sin(ang).astype(np.float32).T
    d["cosT"] = np.ascontiguousarray(np.concatenate([c, c], axis=0))
    d["sinT"] = np.ascontiguousarray(np.concatenate([-s, s], axis=0))
    d["flag"] = np.full((128, 1), float(half), np.float32)
    return d


_NC_CACHE = {}


def kernel(x, norms, ffn_w_gate_up, ffn_w_down, dil_w_in, dil_w_out, diff_w_in, diff_w_out,
           diff_lambda, diff_subln, _stop=None, _dbg=None, _snap=False):
    import time as _time
    _t0 = _time.time()
    if (_stop, _dbg, _snap) not in _NC_CACHE:
        _NC_CACHE[(_stop, _dbg, _snap)] = build(_stop, _dbg, _snap)
    nc = _NC_CACHE[(_stop, _dbg, _snap)]
    _t1 = _time.time()
    sh = _prep_shared(norms, ffn_w_gate_up, ffn_w_down, dil_w_in, dil_w_out, diff_w_in, diff_w_out,
                      diff_lambda, diff_subln)
    if _dbg is not None:
        keep = ("dil" if _dbg.startswith("dil") else "dif")
        sh = {k: v for k, v in sh.items() if not (k.startswith("gu") or k.startswith("dn") or (k[:3] in ("dil", "dif") and not k.startswith(keep)))}
    in_maps = []
    for core in range(8):
        d = dict(sh)
        d.update(_prep_core(x, core // 2, core % 2))
        in_maps.append(d)
    _t2 = _time.time()
    res = run_bass_kernel_spmd(nc, in_maps, core_ids=list(range(8)))
    _t3 = _time.time()
    print(f"[kernel] build {_t1 - _t0:.1f}s prep {_t2 - _t1:.1f}s run {_t3 - _t2:.1f}s", flush=True)
    out = np.empty((4, 2 * T, D), np.float32)
    for core in range(8):
        oT = np.asarray(res.results[core]["outT"]).reshape(D, T)
        out[core // 2, (core % 2) * T:(core % 2 + 1) * T, :] = oT.T
    if _snap:
        snaps = {}
        for nm_ in ("ffn00", "mix0", "layer0", "ffn10", "mix1"):
            o2 = np.empty((2 * T, D), np.float32)
            for core in range(2):
                o2[core * T:(core + 1) * T] = np.asarray(res.results[core]["snap_" + nm_]).reshape(D, T).T
            snaps[nm_] = o2
        return out, snaps
    return out
```

```python
import math
from bisect import bisect_right
from collections import defaultdict

import numpy as np
import concourse.bass as bass
import concourse.mybir as mybir
from concourse.bass_utils import run_bass_kernel_spmd

F32 = mybir.dt.float32
BF16 = mybir.dt.bfloat16
AF = mybir.ActivationFunctionType
ALU = mybir.AluOpType

D = 2048
NC = 16
T = 1024
TB = 512
NTB = 2
DFF = 5632
NJ = 44
HD = 128
EPS = 1e-6
SCALE = HD ** -0.5
SAME_ENGINE_SYNC = True
NSLOT = 3
SEM_EPOCH = 1500
SLOT_BYTES = 8192


class View:
    __slots__ = ("ap", "space", "ivals")

    def __init__(self, ap, space, ivals):
        self.ap = ap
        self.space = space
        self.ivals = ivals


def _ivals(shape, idx, itemsize, base):
    n = len(shape)
    idx = tuple(idx) + (slice(None),) * (n - len(idx))
    rngs = []
    for d, s in zip(shape, idx):
        if isinstance(s, int):
            rngs.append((s, s + 1, 1))
        else:
            a, b, st = s.indices(d)
            rngs.append((a, b, st))
    strides = [1] * n
    for i in range(n - 2, -1, -1):
        strides[i] = strides[i + 1] * shape[i + 1]
    k = n - 1
    while k > 0 and rngs[k] == (0, shape[k], 1):
        k -= 1
    out = []

    def rec(dim, off):
        a, b, st = rngs[dim]
        if dim == k:
            last = a + ((b - a - 1) // st) * st
            lo = off + a * strides[dim]
            hi = off + last * strides[dim] + strides[dim]
            out.append((lo, hi))
            return
        for v in range(a, b, st):
            rec(dim + 1, off + v * strides[dim])

    rec(0, 0)
    out.sort()
    merged = []
    for lo, hi in out:
        if merged and lo <= merged[-1][1]:
            merged[-1][1] = max(merged[-1][1], hi)
        else:
            merged.append([lo, hi])
    return [(base + lo * itemsize, base + hi * itemsize) for lo, hi in merged]


class Buf:
    def __init__(self, space, base_ap, lo, shape, dtype):
        self.space = space
        self.lo = lo
        self.shape = tuple(shape)
        self.itemsize = 4 if dtype == F32 else 2
        n = 1
        for s in shape:
            n *= s
        self.nbytes = n * self.itemsize
        ap = base_ap
        if len(shape) == 2:
            ap = ap.rearrange("p (a b) -> p a b", a=shape[0], b=shape[1])
        elif len(shape) == 3:
            ap = ap.rearrange("p (a b c) -> p a b c", a=shape[0], b=shape[1], c=shape[2])
        self.ap = ap

    def __getitem__(self, idx):
        if not isinstance(idx, tuple):
            idx = (idx,)
        ap = self.ap[(slice(None),) + idx]
        return View(ap, self.space, _ivals(self.shape, idx, self.itemsize, self.lo))

    def all(self):
        return View(self.ap, self.space, [(self.lo, self.lo + self.nbytes)])


class Op:
    __slots__ = ("eng", "fn", "deps", "kind", "signal", "sem", "count", "idx")

    def __init__(self, eng, fn, deps, kind, idx):
        self.eng = eng
        self.fn = fn
        self.deps = deps
        self.kind = kind
        self.signal = False
        self.sem = None
        self.count = 0
        self.idx = idx


class Prog:
    def __init__(self):
        self.ops = []
        self.recs = defaultdict(list)
        self.starts = defaultdict(list)
        self.dry = False

    def _access(self, space, lo, hi, i, key, is_write, deps):
        recs = self.recs[space]
        starts = self.starts[space]
        k = bisect_right(starts, lo) - 1
        if k < 0 or recs[k][1] <= lo:
            k += 1
        j = k
        while j < len(recs) and recs[j][0] < hi:
            r = recs[j]
            if r[2] is not None:
                deps.add(r[2])
            if is_write:
                deps.update(r[3].values())
            j += 1
        over = recs[k:j]
        pieces = []
        if is_write:
            if over and over[0][0] < lo:
                pieces.append([over[0][0], lo, over[0][2], dict(over[0][3])])
            pieces.append([lo, hi, i, {}])
            if over and over[-1][1] > hi:
                pieces.append([hi, over[-1][1], over[-1][2], dict(over[-1][3])])
        else:
            cur = lo
            for r in over:
                if r[0] > cur:
                    pieces.append([cur, r[0], None, {key: i}])
                r[3][key] = i
                pieces.append(r)
                cur = max(cur, r[1])
            if cur < hi:
                pieces.append([cur, hi, None, {key: i}])
        recs[k:j] = pieces
        starts[k:j] = [p[0] for p in pieces]

    def _record(self, eng, fn, reads, writes, kind):
        if self.dry:
            return
        i = len(self.ops)
        key = eng if kind == "c" else ("d", i)
        deps = set()
        for v in reads:
            for lo, hi in v.ivals:
                self._access(v.space, lo, hi, i, key, False, deps)
        for v in writes:
            for lo, hi in v.ivals:
                self._access(v.space, lo, hi, i, key, True, deps)
        deps.discard(i)
        self.ops.append(Op(eng, fn, deps, kind, i))

    def op(self, eng, fn, reads=(), writes=()):
        self._record(eng, fn, reads, writes, "c")

    def dma(self, q, out, in_, **kw):
        self._record(q, lambda e, o=out.ap, i=in_.ap, kw=kw: e.dma_start(out=o, in_=i, **kw), [in_], [out], "d")

    def emit(self, nc, block_cm, sems, ring_sems):
        ops = self.ops
        engs = {"pe": "tensor", "act": "scalar", "dve": "vector", "pool": "gpsimd", "sp": "sync"}
        qn = defaultdict(int)
        for o in ops:
            if o.kind in ("d", "cc"):
                ring = ring_sems[o.eng if o.kind == "d" else "cc"]
                n = qn[(o.eng, o.kind)]
                qn[(o.eng, o.kind)] += 1
                o.sem = ring[n % len(ring)]
                inc = 16 if o.kind == "d" else 1
                o.count = inc * (n // len(ring) + 1)
                o.signal = True
        for o in ops:
            for j in o.deps:
                d = ops[j]
                if d.kind == "c":
                    if d.eng == o.eng and o.kind == "c" and not (SAME_ENGINE_SYNC and d.eng != "pe"):
                        continue
                    d.signal = True
        cnt = defaultdict(int)
        for o in ops:
            if o.kind == "c" and o.signal:
                ep = cnt[o.eng] // SEM_EPOCH
                o.sem = sems[o.eng][ep]
                o.count = cnt[o.eng] % SEM_EPOCH + 1
                cnt[o.eng] += 1
        per = defaultdict(list)
        for o in ops:
            per[o.eng].append(o)
        self.stats = {"nops": len(ops), "cnt": dict(cnt), "per": {k: len(v) for k, v in per.items()},
                      "dmacnt": {k: v for k, v in qn.items()}}

        def make(engname):
            lst = per.get(engname, [])

            def body(e):
                waited = {}

                def wait(sem, val):
                    k = id(sem)
                    if waited.get(k, 0) >= val:
                        return
                    waited[k] = val
                    e.wait_ge(sem, val)

                for o in lst:
                    if o.kind in ("d", "cc"):
                        inc = 16 if o.kind == "d" else 1
                        if o.count > inc:
                            wait(o.sem, o.count - inc)
                    for j in sorted(o.deps):
                        d = ops[j]
                        if d.kind == "c" and d.eng == o.eng and o.kind == "c" and not (
                                SAME_ENGINE_SYNC and d.eng != "pe"):
                            continue
                        wait(d.sem, d.count)
                    if o.fn is None:
                        continue
                    ins = o.fn(e)
                    if o.signal:
                        if o.kind == "d":
                            ins.then_inc(o.sem, 16)
                        else:
                            ins.then_inc(o.sem, 1)
            return body

        with block_cm as block:
            for name, attr in engs.items():
                if per.get(name):
                    getattr(block, attr)(make(name))


LAM_INIT = 0.8 - 0.6 * math.exp(-0.3 * 1)


def build(stop=None, dbg=None, snap=False):
    nc = bass.Bass("TRN2", target_bir_lowering=False)
    P = Prog()

    def dram_in(name, shape, dt=F32):
        return nc.dram_tensor(name, list(shape), dt, kind="ExternalInput").ap()

    xT_d = dram_in("xT", [NC, 128, T])
    gains_d = dram_in("gains", [128, 12 * NC])
    cos_d = dram_in("cosT", [128, T])
    sin_d = dram_in("sinT", [128, T])
    flag_d = dram_in("flag", [128, 1])
    masks_d = dram_in("masks", [128, 4 * 128])
    perm_d = dram_in("perm", [128, 128])
    lam_d = dram_in("lamb", [128, 4 * 128])
    subln_d = dram_in("subln", [128, 2])
    if dbg is None:
        gu_d = [[dram_in(f"gu{l}{f}", [NJ, 128, 4096]) for f in range(2)] for l in range(2)]
        dn_d = [[dram_in(f"dn{l}{f}", [32, 128, 2816]) for f in range(2)] for l in range(2)]
    if dbg is None or dbg.startswith("dil"):
        dil_in_d = dram_in("dil_in", [56, 128, 4096])
        dil_out_d = dram_in("dil_out", [8, 128, 4096])
    if dbg is None or dbg.startswith("dif"):
        dif_in_d = dram_in("dif_in", [24, 128, 4096])
        dif_out_d = dram_in("dif_out", [8, 128, 4096])
    out_d = nc.dram_tensor("outT", [NC, 128, T], F32, kind="ExternalOutput").ap()
    snap_d = {}
    if snap:
        for nm_ in ("ffn00", "mix0", "layer0", "ffn10", "mix1"):
            snap_d[nm_] = nc.dram_tensor("snap_" + nm_, [NC, 128, T], F32, kind="ExternalOutput").ap()

    q1_d = nc.dram_tensor("q1_d", [48 * 128, T], BF16)
    k1_loc = nc.dram_tensor("k1_loc", [48 * 128, T], BF16)
    k1_all = [nc.dram_tensor(f"k1_all{i}", [2 * 512, T], BF16) for i in range(12)]
    v1_loc = nc.dram_tensor("v1_loc", [32 * 128, T], BF16)
    v1_all = [nc.dram_tensor(f"v1_all{i}", [2 * 512, T], BF16) for i in range(8)]
    q2_d = nc.dram_tensor("q2_d", [16 * 128, T], BF16)
    k2_loc = nc.dram_tensor("k2_loc", [16 * 128, T], BF16)
    k2_all = [nc.dram_tensor(f"k2_all{i}", [2 * 512, T], BF16) for i in range(4)]
    v2_loc = nc.dram_tensor("v2_loc", [8 * 128, 2048], BF16)
    v2_all = [nc.dram_tensor(f"v2_all{i}", [2 * 256, 2048], BF16) for i in range(4)]

    def dview(t, ap, lo=0, hi=1 << 40):
        return View(ap, "dram:" + t.name, [(lo, hi)])

    ARENA_BYTES = 207 * 1024
    cm_arena = nc.sbuf_tensor("arena", [128, ARENA_BYTES // 2], BF16)
    arena = cm_arena.__enter__()
    cms = [cm_arena]
    psum = []
    for b in range(8):
        cm = nc.psum_tensor(f"ps{b}", [128, 512], F32)
        psum.append(cm.__enter__())
        cms.append(cm)
    sems = {}
    for e, n in (("pe", 6), ("act", 4), ("dve", 5), ("pool", 2)):
        sems[e] = []
        for i in range(n):
            cm = nc.semaphore(f"sem_{e}{i}")
            sems[e].append(cm.__enter__())
            cms.append(cm)
    ring_sems = {}
    for q, n in (("sp", 40), ("pool", 12), ("cc", 24)):
        ring = []
        for i in range(n):
            cm = nc.semaphore(f"dq_{q}{i}")
            ring.append(cm.__enter__())
            cms.append(cm)
        ring_sems[q] = ring

    off = [0]

    def alloc(shape, dt, at=None):
        n = 1
        for s in shape:
            n *= s
        nb = n * (4 if dt == F32 else 2)
        lo = off[0] if at is None else at
        if at is None:
            off[0] += (nb + 31) // 32 * 32
        base = arena[:, lo // 2:(lo + nb) // 2]
        if dt == F32:
            base = base.bitcast(F32)
        return Buf("sb", base, lo, shape, dt)

    X = alloc([NC, T], F32)
    A = alloc([NC, TB], BF16)
    RB = off[0]
    off[0] += 44 * 1024
    RC = off[0]
    off[0] += 32 * 1024
    slots = [alloc([4096], BF16) for _ in range(NSLOT)]
    gains = alloc([12 * NC], F32)
    cosT = alloc([T], F32)
    sinT = alloc([T], F32)
    flag = alloc([1], F32)
    masks = alloc([4, 128], BF16)
    perm = alloc([128], F32)
    ones_f = alloc([128], F32)
    ones_b = alloc([128], BF16)
    fones_b = alloc([128], BF16)
    rstd = [alloc([TB], F32) for _ in range(2)]
    sq = [alloc([TB], F32) for _ in range(2)]
    tmpf = [alloc([TB], F32) for _ in range(3)]
    masks_f = alloc([4 * 128], F32, at=tmpf[0].lo)
    lamb = alloc([4, 128], F32, at=tmpf[1].lo)
    lamw = alloc([8], F32)
    subln = alloc([2], F32)
    assert off[0] <= ARENA_BYTES, off[0]
    HID = alloc([NJ, TB], BF16, at=RB)
    Y = alloc([NC, TB], F32, at=RC)
    YB = alloc([NC, TB], F32, at=RB)
    OT = alloc([NC, TB], BF16, at=RB + 60 * 1024)
    STG = alloc([4, TB], BF16, at=RB)
    VSTG = alloc([2, 4, 256], BF16, at=RB + 4096)

    PS = [Buf("ps", psum[b][:, :], b * 2048, [512], F32) for b in range(8)]

    class WS:
        def __init__(self):
            self.order = []
            self.dry = True
            self.pos = 0
            self.issued = 0

        def get(self, dram_ap, tname, n):
            if self.dry:
                self.order.append((dram_ap, tname, n))
                return slots[0]
            i = self.pos
            self.pos += 1
            while self.issued < min(len(self.order), i + NSLOT):
                ap, tn, nn = self.order[self.issued]
                s = slots[self.issued % NSLOT]
                P.dma("pool", s[0:nn], View(ap, "dram:" + tn, [(0, 1)]))
                self.issued += 1
            return slots[i % NSLOT]

    ws = WS()
    rr = defaultdict(int)

    def rot(name, lst):
        v = lst[rr[name] % len(lst)]
        rr[name] += 1
        return v

    MM_BANKS = [0, 1, 2, 3]

    def mm(out, lhsT, rhs, start, stop):
        P.op("pe", lambda e, o=out.ap, l=lhsT.ap, r=rhs.ap, s=start, t=stop: e.matmul(o, lhsT=l, rhs=r, start=s, stop=t),
             reads=[lhsT, rhs], writes=[out])

    def act(out, in_, func, reads_extra=(), **kw):
        kws = {k: (v.ap if isinstance(v, View) else v) for k, v in kw.items()}
        rd = [in_] + [v for v in kw.values() if isinstance(v, View)] + list(reads_extra)
        P.op("act", lambda e, o=out.ap, i=in_.ap, f=func, kws=kws: e.activation(out=o, in_=i, func=f, **kws),
             reads=rd, writes=[out])

    def tt(out, in0, in1, op, eng="dve"):
        P.op(eng, lambda e, o=out.ap, a=in0.ap, b=in1.ap, op=op: e.tensor_tensor(out=o, in0=a, in1=b, op=op),
             reads=[in0, in1], writes=[out])

    def ts(out, in0, s1, s2, op0, op1=None, eng="dve"):
        s1a = s1.ap if isinstance(s1, View) else s1
        s2a = s2.ap if isinstance(s2, View) else s2
        rd = [in0] + [v for v in (s1, s2) if isinstance(v, View)]
        if op1 is None:
            P.op(eng, lambda e, o=out.ap, a=in0.ap: e.tensor_single_scalar(out=o, in_=a, scalar=s1a, op=op0),
                 reads=rd, writes=[out])
        else:
            P.op(eng, lambda e, o=out.ap, a=in0.ap: e.tensor_scalar(out=o, in0=a, scalar1=s1a, scalar2=s2a, op0=op0, op1=op1),
                 reads=rd, writes=[out])

    def stt(out, in0, scalar, in1, op0, op1, eng="dve"):
        sa = scalar.ap if isinstance(scalar, View) else scalar
        rd = [in0, in1] + ([scalar] if isinstance(scalar, View) else [])
        P.op(eng, lambda e, o=out.ap, a=in0.ap, b=in1.ap: e.scalar_tensor_tensor(out=o, in0=a, scalar=sa, in1=b, op0=op0, op1=op1),
             reads=rd, writes=[out])

    def tsl(t):
        return slice(t * TB, (t + 1) * TB)

    def gcol(l, i, c):
        n = (l * 6 + i) * NC + c
        return gains[n:n + 1]

    def prologue():
        for c in range(NC):
            P.dma("sp", X[c], dview_in(xT_d[c], "xT"))
        P.dma("sp", gains.all(), dview_in(gains_d, "gains"))
        P.dma("sp", cosT.all(), dview_in(cos_d, "cos"))
        P.dma("sp", sinT.all(), dview_in(sin_d, "sin"))
        P.dma("sp", flag.all(), dview_in(flag_d, "flag"))
        P.dma("sp", masks_f.all(), dview_in(masks_d, "masks"))
        P.dma("sp", perm.all(), dview_in(perm_d, "perm"))
        P.dma("sp", lamb.all(), dview_in(lam_d, "lamb"))
        P.dma("sp", subln.all(), dview_in(subln_d, "subln"))
        P.op("dve", lambda e: e.memset(ones_f.ap, 1.0), writes=[ones_f.all()])
        P.op("dve", lambda e: e.memset(ones_b.ap, 1.0), writes=[ones_b.all()])
        ts(fones_b.all(), ones_f.all(), flag[0:1], None, ALU.mult)
        P.op("dve", lambda e: e.tensor_copy(out=masks.ap, in_=masks_f.ap.rearrange("p (a b) -> p a b", a=4, b=128)),
             reads=[masks_f.all()], writes=[masks.all()])
        ts(gains.all(), gains.all(), math.sqrt(D), None, ALU.mult)
        for l in range(2):
            for i in (1, 5):
                n = (l * 6 + i) * NC
                ts(gains[n:n + NC], gains[n:n + NC], 0.5, None, ALU.mult)
        for a in range(2):
            tt(tmpf[a][0:128], lamb[2 * a], lamb[2 * a + 1], ALU.mult)
            P.op("dve", lambda e, o=lamw[a:a + 1].ap, i=tmpf[a][0:128].ap: e.reduce_sum(out=o, in_=i, axis=mybir.AxisListType.X),
                 reads=[tmpf[a][0:128]], writes=[lamw[a:a + 1]])
            act(lamw[2 + a:3 + a], lamw[a:a + 1], AF.Exp)
        tt(lamw[4:5], lamw[3:4], lamw[2:3], ALU.subtract)
        ts(lamw[5:6], lamw[4:5], -LAM_INIT, None, ALU.add)
        ts(subln.all(), subln.all(), 16.0 * (1.0 - LAM_INIT), None, ALU.mult)

    def dview_in(ap, name):
        return View(ap, "dram:in_" + name, [(0, 1)])

    def stats_finish(ps_stat, r, dim, eps):
        act(r.all(), ps_stat.all(), AF.Sqrt, bias=float(dim * eps), scale=1.0)
        P.op("dve", lambda e, o=r.ap: e.reciprocal(out=o, in_=o), reads=[r.all()], writes=[r.all()])

    def sq_accumulate(r, src, first):
        if first:
            act(r.all(), src, AF.Square)
        else:
            s = rot("sq", sq)
            act(s.all(), src, AF.Square)
            tt(r.all(), r.all(), s.all(), ALU.add)

    pre_state = {}

    def prenorm_a(t, l, i):
        r = rot("rstd", rstd)
        pre_state[(t, l, i)] = r
        for c in range(NC):
            sq_accumulate(r, X[c, tsl(t)], c == 0)

    def prenorm_b(t, l, i):
        ps_stat = PS[4]
        r = pre_state.pop((t, l, i))
        mm(ps_stat.all(), ones_f.all(), r.all(), True, True)
        stats_finish(ps_stat, r, D, EPS)
        for c in range(NC):
            stt(A[c], X[c, tsl(t)], gcol(l, i, c), r.all(), ALU.mult, ALU.mult)

    def prenorm(t, l, i):
        prenorm_a(t, l, i)
        prenorm_b(t, l, i)

    norm_state = {}

    def evac_y(ps, Yb, c, l, i, ps_stat):
        act(Yb[c], ps.all(), AF.Copy, scale=gcol(l, i, c))
        if c == 0:
            norm_state["r"] = rot("rstd", rstd)
        sq_accumulate(norm_state["r"], ps.all(), c == 0)

    def postnorm(t, Yb, ps_stat):
        r = norm_state["r"]
        mm(ps_stat.all(), ones_f.all(), r.all(), True, True)
        stats_finish(ps_stat, r, D, EPS)
        for c in range(NC):
            tmp = rot("tmpf", tmpf)
            tt(tmp.all(), Yb[c], r.all(), ALU.mult)
            tt(X[c, tsl(t)], X[c, tsl(t)], tmp.all(), ALU.add)

    def ffn_ph1(t, l, f):
        for j in range(NJ):
            slot = ws.get(gu_d[l][f][j], f"gu{l}{f}", 4096)
            W = Buf("sb", slot.ap, slot.lo, [NC, 256], BF16)
            pg = PS[rot("mm", MM_BANKS)]
            pu = PS[rot("mm", MM_BANKS)]
            for k in range(NC):
                mm(pg.all(), W[k, 0:128], A[k], k == 0, k == NC - 1)
            for k in range(NC):
                mm(pu.all(), W[k, 128:256], A[k], k == 0, k == NC - 1)
            sl = rot("tmpf", tmpf)
            act(sl.all(), pg.all(), AF.Silu)
            tt(HID[j], sl.all(), pu.all(), ALU.mult)

    def ffn_ph2(t, l, f, mid=None):
        ps_stat = PS[5]
        for c in range(NC):
            if c == 3 and mid is not None:
                mid()
            py = PS[rot("mm", MM_BANKS)]
            for half in range(2):
                slot = ws.get(dn_d[l][f][c * 2 + half], f"dn{l}{f}", 2816)
                W = Buf("sb", slot.ap[:, 0:2816], slot.lo, [22, 128], BF16)
                for jj in range(22):
                    mm(py.all(), W[jj], HID[half * 22 + jj], half == 0 and jj == 0, half == 1 and jj == 21)
            evac_y(py, Y, c, l, 1 if f == 0 else 5, ps_stat)
        postnorm(t, Y, ps_stat)

    def ffn_seq(l, f, next_pre):
        pi = 0 if f == 0 else 4
        ffn_ph1(0, l, f)
        prenorm_a(1, l, pi)
        ffn_ph2(0, l, f, mid=lambda: prenorm_b(1, l, pi))
        ffn_ph1(1, l, f)
        if next_pre is not None:
            prenorm_a(*next_pre)
            ffn_ph2(1, l, f, mid=lambda: prenorm_b(*next_pre))
        else:
            ffn_ph2(1, l, f)

    def perm_out(stg_view_ap, g):
        if g == 0:
            return stg_view_ap, None
        return stg_view_ap.rearrange("p (r m) -> p m r", r=4, m=128), ("p (m r) -> p m r", dict(m=128, r=4))

    def qk_tile_finish(ps, t, g, dst_dram, dst_t, row0):
        qf = rot("tmpf", tmpf)
        act(qf.all(), ps.all(), AF.Copy)
        ps2 = PS[rot("swap", [6, 7])]
        mm(ps2.all(), perm.all(), qf.all(), True, True)
        t1 = rot("tmpf", tmpf)
        tt(t1.all(), qf.all(), cosT[tsl(t)], ALU.mult, eng="pool")
        t2 = rot("sq", sq)
        tt(t2.all(), ps2.all(), sinT[tsl(t)], ALU.mult)
        si = rr["stg"] % 4
        rr["stg"] += 1
        sv = STG[si]
        oap, inre = perm_out(sv.ap, g)
        if inre is None:
            a_ap, b_ap = t1.ap, t2.ap
        else:
            a_ap = t1.ap.rearrange(inre[0], **inre[1])
            b_ap = t2.ap.rearrange(inre[0], **inre[1])
        P.op("dve", lambda e, o=oap, a=a_ap, b=b_ap: e.tensor_tensor(out=o, in0=a, in1=b, op=ALU.add),
             reads=[t1.all(), t2.all()], writes=[sv])
        P.dma("sp", dview(dst_t, dst_dram[row0:row0 + 128, tsl(t)], (row0 * 2 + t) * 1, (row0 * 2 + t) * 1 + 1),
              sv)

    def v_cols(k, g, b):
        base = A[k]
        if g == 0:
            return A[k, b * 128:(b + 1) * 128]
        ap = base.ap.rearrange("p (m r) -> p r m", m=128, r=4)[:, b, :]
        return View(ap, "sb", base.ivals)

    def v_tile(slot, t, vt, v_loc, dil):
        W = Buf("sb", slot.ap, slot.lo, [NC, 256], BF16)
        for g in range(2 if dil else 1):
            vi = rr["vstg"] % 2
            rr["vstg"] += 1
            for pair in range(2):
                ps = PS[rot("mm", MM_BANKS)]
                for tb2 in range(2):
                    tb = pair * 2 + tb2
                    for k in range(NC):
                        mm(ps[tb2 * 256:(tb2 + 1) * 256], v_cols(k, g, tb), W[k], k == 0, k == NC - 1)
                P.op("act", lambda e, o=VSTG[vi, pair * 2:pair * 2 + 2].ap, i=ps.ap.rearrange("p (a b) -> p a b", a=2, b=256):
                     e.activation(out=o, in_=i, func=AF.Copy),
                     reads=[ps.all()], writes=[VSTG[vi, pair * 2:pair * 2 + 2]])
            if dil:
                for hh in range(2):
                    h = vt * 2 + hh
                    r0 = (g * 16 + h) * 128
                    dst = v_loc.ap()[r0:r0 + 128, t * TB:(t + 1) * TB].rearrange("p (b f) -> p b f", f=128)
                    src = VSTG[vi]
                    src = View(src.ap[:, :, hh * 128:(hh + 1) * 128], src.space, src.ivals)
                    P.dma("sp", dview(v_loc, dst, 100000 + r0 * 2 + t, 100000 + r0 * 2 + t + 1), src)
            else:
                r0 = vt * 128
                dst = v_loc.ap()[r0:r0 + 128, t * 4 * 256:(t + 1) * 4 * 256].rearrange("p (b f) -> p b f", f=256)
                P.dma("sp", dview(v_loc, dst, 100000 + r0 * 2 + t, 100000 + r0 * 2 + t + 1), VSTG[vi])

    def mixer_pass1(t, l, w_in_d, wname, n_qk_tiles, groups_of_tile, q_d, k_loc, v_loc, part="all", pre=True):
        if pre:
            prenorm(t, l, 2)
        pending = []
        tis = {"all": range(2 * n_qk_tiles), "q": range(n_qk_tiles), "kv": range(n_qk_tiles, 2 * n_qk_tiles)}[part]
        for ti in tis:
            slot = ws.get(w_in_d[ti], wname, 4096)
            W = Buf("sb", slot.ap, slot.lo, [NC, 256], BF16)
            isq = ti < n_qk_tiles
            tl = ti if isq else ti - n_qk_tiles
            for m in range(2):
                ps = PS[rot("mm", MM_BANKS)]
                for k in range(NC):
                    mm(ps.all(), W[k, m * 128:(m + 1) * 128], A[k], k == 0, k == NC - 1)
                row0 = (tl * 2 + m) * 128
                g = groups_of_tile(tl)
                dst_t = q_d if isq else k_loc
                for fn in pending:
                    fn()
                pending = [lambda ps=ps, g=g, dst_t=dst_t, row0=row0: qk_tile_finish(ps, t, g, dst_t.ap(), dst_t, row0)]
        for fn in pending:
            fn()
        if part == "q":
            return
        for vt in range(8):
            slot = ws.get(w_in_d[2 * n_qk_tiles + vt], wname, 4096)
            v_tile(slot, t, vt, v_loc, n_qk_tiles == 24)

    def exchange(k_loc, k_all, v_loc, v_all):
        nk = len(k_all)
        order = []
        for fc in range(4):
            if nk == 12:
                order += [("k", fc), ("k", 4 + fc), ("k", 8 + fc), ("v", fc), ("v", 4 + fc)]
            else:
                order += [("k", fc), ("v", fc)]
        vrows = 512 if nk == 12 else 256
        for kind, i in order:
            loc, a, rows = (k_loc, k_all[i], 512) if kind == "k" else (v_loc, v_all[i], vrows)
            P._record("pool", lambda e, x=loc.ap()[i * rows:(i + 1) * rows, :].opt(), y=a.ap().opt(): e.collective_compute(
                "AllGather", ALU.bypass, replica_groups=[[0, 1], [2, 3], [4, 5], [6, 7]], ins=[x], outs=[y]),
                [dview(loc, None)], [dview(a, None)], "cc")

    def attn_out_proj(t, l, w_out_d, wname):
        ps_stat = PS[5]
        for ti in range(8):
            slot = ws.get(w_out_d[ti], wname, 4096)
            W = Buf("sb", slot.ap, slot.lo, [NC, 256], BF16)
            for m in range(2):
                c = ti * 2 + m
                py = PS[rot("mm", MM_BANKS)]
                for k in range(NC):
                    mm(py.all(), W[k, m * 128:(m + 1) * 128], OT[k], k == 0, k == NC - 1)
                evac_y(py, YB, c, l, 3, ps_stat)
        postnorm(t, YB, ps_stat)

    AB = RB

    def dil_attention(t):
        nbk = (t + 1) * 4
        bufs = []
        o = AB
        for i in range(2):
            b = {}
            b["q"] = alloc([3, TB], BF16, at=o); o += 3 * 1024
            b["ko"] = alloc([3, T], BF16, at=o); o += 6 * 1024
            b["kp"] = alloc([3, T], BF16, at=o); o += 6 * 1024
            b["vo"] = alloc([2, 8, 128], BF16, at=o); o += 4 * 1024
            b["vp"] = alloc([2, 8, 128], BF16, at=o); o += 4 * 1024
            bufs.append(b)
        assert o <= RB + 54 * 1024
        EB = [alloc([TB], BF16, at=RB + 54 * 1024 + i * 1024) for i in range(6)]
        RZ = alloc([TB], F32, at=RB + 50 * 1024)

        def load_head(h, b):
            def own(tn, ncols, ng):
                return tn.ap().rearrange("(g h p) c -> p g h c", g=ng, h=16)[:, :, h, 0:ncols]
            P.dma("sp", b["q"].all(), dview(q1_d, q1_d.ap().rearrange("(g h p) c -> p g h c", g=3, h=16)[:, :, h, tsl(t)]))
            nt = (t + 1) * TB
            P.dma("sp", b["ko"][:, 0:nt], dview(k1_loc, own(k1_loc, nt, 3)))
            kcols = {0: (896, 1024) if t == 0 else None, 1: (512, 1024) if t == 0 else None, 2: (0, 1024)}
            for g in range(3):
                if kcols[g] is None:
                    continue
                c0, c1 = kcols[g]
                sl_ = g * 16 + h
                ka = k1_all[sl_ // 4]
                rs = slice((sl_ % 4) * 128, (sl_ % 4 + 1) * 128)
                P.dma("sp", b["kp"][g, c0:c1], dview(ka, ka.ap()[rs, c0:c1]))
            P.dma("sp", b["vo"][:, 0:(t + 1) * 4], dview(v1_loc, own(v1_loc, nt, 2).rearrange("p g (b f) -> p g b f", f=128)))
            vblk = {0: (7, 8) if t == 0 else None, 1: (0, 8)}
            for g in range(2):
                if vblk[g] is None:
                    continue
                b0, b1 = vblk[g]
                sl_ = g * 16 + h
                va = v1_all[sl_ // 4]
                rs = slice((sl_ % 4) * 128, (sl_ % 4 + 1) * 128)
                P.dma("sp", b["vp"][g, b0:b1], dview(va, va.ap()[rs, b0 * 128:b1 * 128].rearrange("p (b f) -> p b f", f=128)))
                ts(b["vp"][g, b0:b1], b["vp"][g, b0:b1], flag[0:1], None, ALU.mult)

        def pairs_for():
            pl = []
            for i in range(4):
                pl.append((0, i, "o", t * 4 + i, 0))
                if i > 0:
                    pl.append((0, i, "o", t * 4 + i - 1, 1))
                elif t == 1:
                    pl.append((0, i, "o", 3, 1))
                else:
                    pl.append((0, i, "p", 7, 1))
            for r in range(4):
                pl.append((1, r, "o", t * 4 + r, 0))
                if t == 1:
                    pl.append((1, r, "o", r, 1))
                else:
                    pl.append((1, r, "p", 4 + r, 1))
            for r4 in range(4):
                pl.append((2, r4, "p", r4, 2))
                pl.append((2, r4, "p", 4 + r4, 2))
                if t == 1:
                    pl.append((2, r4, "o", r4, 2))
                pl.append((2, r4, "o", t * 4 + r4, 3))
            return pl

        def qcols(ps, g, qb):
            if g == 0:
                v = ps[qb * 128:(qb + 1) * 128]
                return v.ap
            return ps.ap.rearrange("p (m r) -> p r m", m=128, r=4)[:, qb, :]

        pl = pairs_for()
        groups = [pl[i:i + 4] for i in range(0, len(pl), 4)]
        load_head(0, bufs[0])
        for h in range(16):
            b = bufs[h % 2]
            if h + 1 < 16:
                load_head(h + 1, bufs[(h + 1) % 2])
            pso = PS[rot("dil_o", [2, 3])]
            psz = PS[rot("dil_z", [6, 7])]
            started = set()

            def do_pv(grp, E):
                for i, (g, qb, src, kb, mk) in enumerate(grp):
                    Ev = E[i * 128:(i + 1) * 128]
                    if mk is not None:
                        tt(Ev, Ev, masks[mk], ALU.mult)
                for i, (g, qb, src, kb, mk) in enumerate(grp):
                    Ev = E[i * 128:(i + 1) * 128]
                    st = len(started) == 0
                    started.add((g, qb))
                    V = b["vo"] if src == "o" else b["vp"]
                    on = ones_b if src == "o" else fones_b
                    rhs_ap = Ev.ap
                    vg = min(g, 1)
                    P.op("pe", lambda e, o=qcols(pso, g, qb), l=V[vg, kb].ap, r=rhs_ap, s=st:
                         e.matmul(o, lhsT=l, rhs=r, start=s, stop=False, skip_group_check=True),
                         reads=[V[vg, kb], Ev], writes=[pso.all()])
                    P.op("pe", lambda e, o=qcols(psz, g, qb), l=on.ap, r=rhs_ap, s=st:
                         e.matmul(o, lhsT=l, rhs=r, start=s, stop=False, skip_group_check=True),
                         reads=[on.all(), Ev], writes=[psz.all()])

            pend = []
            for grp in groups:
                pss = PS[rot("dil_s", [0, 1, 4, 5])]
                for i, (g, qb, src, kb, mk) in enumerate(grp):
                    K = b["ko"] if src == "o" else b["kp"]
                    mm(pss[i * 128:(i + 1) * 128], K[g, kb * 128:(kb + 1) * 128], b["q"][g, qb * 128:(qb + 1) * 128], True, True)
                E = rot("EB", EB)
                n = len(grp) * 128
                act(E[0:n], pss[0:n], AF.Exp, scale=SCALE)
                pend.append((grp, E))
                if len(pend) > 3:
                    do_pv(*pend.pop(0))
            while pend:
                do_pv(*pend.pop(0))
            act(RZ.all(), psz.all(), AF.Ln); act(RZ.all(), RZ.all(), AF.Exp, scale=-1.0)
            tt(OT[h], pso.all(), RZ.all(), ALU.mult)

    def diff_attention(t):
        nown = (t + 1) * 4
        bufs = []
        o = AB
        for i in range(2):
            b = {}
            b["q"] = alloc([2, TB], BF16, at=o); o += 2 * 1024
            b["ko"] = alloc([2, T], BF16, at=o); o += 4 * 1024
            b["kp"] = alloc([2, T], BF16, at=o); o += 4 * 1024
            b["vo"] = alloc([8, 256], BF16, at=o); o += 4 * 1024
            b["vp"] = alloc([8, 256], BF16, at=o); o += 4 * 1024
            bufs.append(b)
        O1 = alloc([2, TB], F32, at=o); o += 4096
        RS = alloc([TB], F32, at=o); o += 2048
        assert o <= RB + 54 * 1024
        EB = [alloc([TB], BF16, at=RB + 54 * 1024 + i * 1024) for i in range(6)]
        RZ = alloc([TB], F32, at=RB + 50 * 1024)

        def load_head(h, b):
            nt = (t + 1) * TB
            P.dma("sp", b["q"].all(), dview(q2_d, q2_d.ap().rearrange("(h c p) t -> p h c t", h=8, c=2)[:, h, :, tsl(t)]))
            P.dma("sp", b["ko"][:, 0:nt], dview(k2_loc, k2_loc.ap().rearrange("(h c p) t -> p h c t", h=8, c=2)[:, h, :, 0:nt]))
            ka = k2_all[h // 2]
            P.dma("sp", b["kp"].all(), dview(ka, ka.ap()[(h % 2) * 256:(h % 2 + 1) * 256, :].rearrange("(c p) t -> p c t", c=2)))
            P.dma("sp", b["vo"][0:nown], dview(v2_loc, v2_loc.ap()[h * 128:(h + 1) * 128, 0:nown * 256].rearrange("p (b f) -> p b f", f=256)))
            va = v2_all[h // 2]
            P.dma("sp", b["vp"].all(), dview(va, va.ap()[(h % 2) * 128:(h % 2 + 1) * 128, :].rearrange("p (b f) -> p b f", f=256)))
            ts(b["vp"].all(), b["vp"].all(), flag[0:1], None, ALU.mult)

        kbl = [("p", kb) for kb in range(8)] + [("o", kb) for kb in range(nown)]
        load_head(0, bufs[0])
        for h in range(8):
            b = bufs[h % 2]
            if h + 1 < 8:
                load_head(h + 1, bufs[(h + 1) % 2])
            for c in range(2):
                psu = [PS[2], PS[3]] if c == 0 else [PS[6], PS[7]]
                psz = PS[4]

                def do_pv(item, E):
                    src, kb, q0, first = item
                    n = TB - q0
                    if src == "o" and kb >= t * 4:
                        tt(E[0:128], E[0:128], masks[0], ALU.mult)
                    V = b["vo"] if src == "o" else b["vp"]
                    on = ones_b if src == "o" else fones_b
                    for vc in range(2):
                        P.op("pe", lambda e, o=psu[vc][q0:TB].ap, l=V[kb, vc * 128:(vc + 1) * 128].ap, r=E[0:n].ap, s=first:
                             e.matmul(o, lhsT=l, rhs=r, start=s, stop=False, skip_group_check=True),
                             reads=[V[kb, vc * 128:(vc + 1) * 128], E[0:n]], writes=[psu[vc].all()])
                    P.op("pe", lambda e, o=psz[q0:TB].ap, l=on.ap, r=E[0:n].ap, s=first:
                         e.matmul(o, lhsT=l, rhs=r, start=s, stop=False, skip_group_check=True),
                         reads=[on.all(), E[0:n]], writes=[psz.all()])

                pend = []
                for idx, (src, kb) in enumerate(kbl):
                    q0 = 0
                    if src == "o" and kb >= t * 4:
                        q0 = (kb - t * 4) * 128
                    n = TB - q0
                    pss = PS[rot("dif_s", [0, 1, 5])]
                    K = b["ko"] if src == "o" else b["kp"]
                    mm(pss[0:n], K[c, kb * 128:(kb + 1) * 128], b["q"][c, q0:TB], True, True)
                    E = rot("EB", EB)
                    act(E[0:n], pss[0:n], AF.Exp, scale=SCALE)
                    pend.append(((src, kb, q0, idx == 0), E))
                    if len(pend) > 2:
                        do_pv(*pend.pop(0))
                while pend:
                    do_pv(*pend.pop(0))
                act(RZ.all(), psz.all(), AF.Ln); act(RZ.all(), RZ.all(), AF.Exp, scale=-1.0)
                if c == 0:
                    for vc in range(2):
                        tt(O1[vc], psu[vc].all(), RZ.all(), ALU.mult)
                else:
                    ts(RZ.all(), RZ.all(), lamw[5:6], None, ALU.mult)
                    for vc in range(2):
                        tmp = rot("tmpf", tmpf)
                        tt(tmp.all(), psu[vc].all(), RZ.all(), ALU.mult)
                        tt(O1[vc], O1[vc], tmp.all(), ALU.add)
            ps_stat = PS[4]
            for vc in range(2):
                s = rot("sq", sq)
                act(s.all(), O1[vc], AF.Square)
                mm(ps_stat.all(), ones_f.all(), s.all(), vc == 0, vc == 1)
            stats_finish(ps_stat, RS, 256, 1e-5)
            for vc in range(2):
                stt(OT[h * 2 + vc], O1[vc], subln[vc:vc + 1], RS.all(), ALU.mult, ALU.mult)

    def program():
        prologue()
        stage = 0

        def done(name):
            if name in snap_d:
                for c in range(NC):
                    P.dma("sp", View(snap_d[name][c], "dram:snap" + name, [(c, c + 1)]), X[c])
            return stop is not None and stop == name

        if dbg is not None:
            sub = dbg.split(":")[1] if ":" in dbg else "all"
            if dbg.startswith("dil"):
                for t in range(NTB):
                    mixer_pass1(t, 0, dil_in_d, "dil_in", 24, lambda tl: (tl * 2) // 16, q1_d, k1_loc, v1_loc)
                if sub == "p1":
                    return
                exchange(k1_loc, k1_all, v1_loc, v1_all)
                if sub == "xc":
                    return
                for t in range(NTB):
                    dil_attention(t)
                    attn_out_proj(t, 0, dil_out_d, "dil_out")
            else:
                for t in range(NTB):
                    mixer_pass1(t, 1, dif_in_d, "dif_in", 8, lambda tl: 0, q2_d, k2_loc, v2_loc)
                exchange(k2_loc, k2_all, v2_loc, v2_all)
                for t in range(NTB):
                    diff_attention(t)
                    attn_out_proj(t, 1, dif_out_d, "dif_out")
            return
        prenorm(0, 0, 0)
        ffn_seq(0, 0, (0, 0, 2))
        if done("ffn00"):
            return
        a1 = (0, dil_in_d, "dil_in", 24, lambda tl: (tl * 2) // 16, q1_d, k1_loc, v1_loc)
        mixer_pass1(0, *a1, part="kv", pre=False)
        mixer_pass1(1, *a1, part="kv")
        exchange(k1_loc, k1_all, v1_loc, v1_all)
        mixer_pass1(1, *a1, part="q", pre=False)
        mixer_pass1(0, *a1, part="q")
        dil_attention(0)
        attn_out_proj(0, 0, dil_out_d, "dil_out")
        prenorm_a(0, 0, 4)
        dil_attention(1)
        prenorm_b(0, 0, 4)
        attn_out_proj(1, 0, dil_out_d, "dil_out")
        if done("mix0"):
            return
        ffn_seq(0, 1, (0, 1, 0))
        if done("layer0"):
            return
        ffn_seq(1, 0, (0, 1, 2))
        if done("ffn10"):
            return
        a2 = (1, dif_in_d, "dif_in", 8, lambda tl: 0, q2_d, k2_loc, v2_loc)
        mixer_pass1(0, *a2, part="kv", pre=False)
        mixer_pass1(1, *a2, part="kv")
        exchange(k2_loc, k2_all, v2_loc, v2_all)
        mixer_pass1(1, *a2, part="q", pre=False)
        mixer_pass1(0, *a2, part="q")
        diff_attention(0)
        attn_out_proj(0, 1, dif_out_d, "dif_out")
        prenorm_a(0, 1, 4)
        diff_attention(1)
        prenorm_b(0, 1, 4)
        attn_out_proj(1, 1, dif_out_d, "dif_out")
        if done("mix1"):
            return
        ffn_seq(1, 1, None)

    ws.dry = True
    P.dry = True
    program()
    ws.dry = False
    P.dry = False
    rr.clear()
    program()
    outs = []
    for c in range(NC):
        P.dma("sp", View(out_d[c], "dram:out", [(c, c + 1)]), X[c])
    P.op("sp", None, reads=[View(None, "dram:out", [(0, NC)])])

    P.emit(nc, nc.Block(), sems, ring_sems)
    build.stats = P.stats
    for cm in reversed(cms):
        cm.__exit__(None, None, None)
    return nc


def _tile_kn(W, ncols=256):
    K, N = W.shape
    nt = N // ncols
    return np.ascontiguousarray(W.reshape(K // 128, 128, nt, ncols).transpose(2, 1, 0, 3)).reshape(nt, 128, (K // 128) * ncols)


def _prep_shared(norms, ffn_w_gate_up, ffn_w_down, dil_w_in, dil_w_out, diff_w_in, diff_w_out, diff_lambda, diff_subln):
    sh = {}
    g = np.asarray(norms, np.float32).reshape(12, NC, 128)
    sh["gains"] = np.ascontiguousarray(g.transpose(2, 0, 1)).reshape(128, 12 * NC)
    for l in range(2):
        for f in range(2):
            W = np.asarray(ffn_w_gate_up[l, f], np.float32)
            Wg = W[:, :DFF].reshape(D, NJ, 128)
            Wu = W[:, DFF:].reshape(D, NJ, 128)
            Wi = np.concatenate([Wg, Wu], axis=2).reshape(D, NJ * 256)
            sh[f"gu{l}{f}"] = _tile_kn(Wi)
            Wd = np.asarray(ffn_w_down[l, f], np.float32)
            Wd4 = Wd.reshape(2, 22, 128, NC, 128)
            sh[f"dn{l}{f}"] = np.ascontiguousarray(Wd4.transpose(3, 0, 2, 1, 4)).reshape(32, 128, 22 * 128)
    sh["dil_in"] = _tile_kn(np.asarray(dil_w_in[0], np.float32))
    sh["dil_out"] = _tile_kn(np.asarray(dil_w_out[0], np.float32))
    sh["dif_in"] = _tile_kn(np.asarray(diff_w_in[0], np.float32))
    sh["dif_out"] = _tile_kn(np.asarray(diff_w_out[0], np.float32))
    lp = np.asarray(diff_lambda[0], np.float32).reshape(1, 4 * 128)
    sh["lamb"] = np.ascontiguousarray(np.broadcast_to(lp, (128, 4 * 128)))
    sh["subln"] = np.ascontiguousarray(np.asarray(diff_subln[0], np.float32).reshape(2, 128).T)
    k = np.arange(128)[:, None]
    q = np.arange(128)[None, :]
    m_causal = (k <= q)
    m_upper = (k >= q)
    same = (k % 4) == (q % 4)
    m_bd = same
    m_bdc = same & ((k // 4) <= (q // 4))
    sh["masks"] = np.concatenate([m_causal, m_upper, m_bd, m_bdc], axis=1).astype(np.float32)
    pm = np.zeros((128, 128), np.float32)
    for m in range(128):
        pm[(m + 64) % 128, m] = 1.0
    sh["perm"] = pm
    return sh


def _prep_core(x, b, half):
    d = {}
    xs = np.asarray(x[b, half * T:(half + 1) * T, :], np.float32)
    d["xT"] = np.ascontiguousarray(xs.T).reshape(NC, 128, T)
    inv_freq = (np.float32(10000.0) ** (-np.arange(0, HD, 2, dtype=np.float32) / np.float32(HD))).astype(np.float32)
    pos = np.arange(half * T, (half + 1) * T, dtype=np.float32)
    ang = (pos[:, None] * inv_freq[None, :]).astype(np.float32)
    c = np.cos(ang).astype(np.float32).T
    s = np.sin(ang).astype(np.float32).T
    d["cosT"] = np.ascontiguousarray(np.concatenate([c, c], axis=0))
    d["sinT"] = np.ascontiguousarray(np.concatenate([-s, s], axis=0))
    d["flag"] = np.full((128, 1), float(half), np.float32)
    return d


_NC_CACHE = {}


def kernel(x, norms, ffn_w_gate_up, ffn_w_down, dil_w_in, dil_w_out, diff_w_in, diff_w_out,
           diff_lambda, diff_subln, _stop=None, _dbg=None, _snap=False):
    import time as _time
    _t0 = _time.time()
    if (_stop, _dbg, _snap) not in _NC_CACHE:
        _NC_CACHE[(_stop, _dbg, _snap)] = build(_stop, _dbg, _snap)
    nc = _NC_CACHE[(_stop, _dbg, _snap)]
    _t1 = _time.time()
    sh = _prep_shared(norms, ffn_w_gate_up, ffn_w_down, dil_w_in, dil_w_out, diff_w_in, diff_w_out,
                      diff_lambda, diff_subln)
    if _dbg is not None:
        keep = ("dil" if _dbg.startswith("dil") else "dif")
        sh = {k: v for k, v in sh.items() if not (k.startswith("gu") or k.startswith("dn") or (k[:3] in ("dil", "dif") and not k.startswith(keep)))}
    in_maps = []
    for core in range(8):
        d = dict(sh)
        d.update(_prep_core(x, core // 2, core % 2))
        in_maps.append(d)
    _t2 = _time.time()
    res = run_bass_kernel_spmd(nc, in_maps, core_ids=list(range(8)))
    _t3 = _time.time()
    print(f"[kernel] build {_t1 - _t0:.1f}s prep {_t2 - _t1:.1f}s run {_t3 - _t2:.1f}s", flush=True)
    out = np.empty((4, 2 * T, D), np.float32)
    for core in range(8):
        oT = np.asarray(res.results[core]["outT"]).reshape(D, T)
        out[core // 2, (core % 2) * T:(core % 2 + 1) * T, :] = oT.T
    if _snap:
        snaps = {}
        for nm_ in ("ffn00", "mix0", "layer0", "ffn10", "mix1"):
            o2 = np.empty((2 * T, D), np.float32)
            for core in range(2):
                o2[core * T:(core + 1) * T] = np.asarray(res.results[core]["snap_" + nm_]).reshape(D, T).T
            snaps[nm_] = o2
        return out, snaps
    return out
```
